# Optimizing a Trainium2 kernel written in Bass

```python
import jax, jax.numpy as jnp
from jax import lax
import numpy as np

D_MODEL = 4096
BATCH = 8
SEQ = 2048
DEPTH = 2

GRID_W = 64
EPS = 1e-6
D_FF = 11008
MIX_WIDTH = 2 * D_MODEL
D_FOURIER = MIX_WIDTH // 4
FOURIER_GROUPS = 8
FOURIER_GROUP_DIM = D_FOURIER // FOURIER_GROUPS
D_SSD = MIX_WIDTH - D_FOURIER
SSD_HEAD_DIM = 64
SSD_HEADS = D_SSD // SSD_HEAD_DIM
SSD_GROUPS = 8
SSD_HEADS_PER_GROUP = SSD_HEADS // SSD_GROUPS
D_STATE = 128
D_CONV = 5
SSD_CHUNK = 128
CONV_CH = D_SSD + 2 * SSD_GROUPS * D_STATE
IN_COLS = D_FOURIER + D_SSD + CONV_CH + 2 * SSD_HEADS
N_HEADS = 32
N_KV_HEADS = 8
KV_GROUP = N_HEADS // N_KV_HEADS
HEAD_DIM = D_MODEL // N_HEADS
ROPE_HALF = HEAD_DIM // 2
ROPE_THETA = 10000.0
Q_BLOCK = 128
QKV_COLS = (N_HEADS + 2 * N_KV_HEADS) * HEAD_DIM
N_EVEN = (DEPTH + 1) // 2
N_ODD = DEPTH // 2

kernel_name = "hybrid_fnet_ssd_axial_gqa_macaron"


def rmsnorm(x, w):
    xf = x.astype(jnp.float32)
    y = xf * lax.rsqrt(jnp.mean(xf * xf, axis=-1, keepdims=True) + EPS)
    return (y * w.astype(jnp.float32)).astype(x.dtype)


def swiglu_ffn(h, w_gate, w_up, w_down):
    return (jax.nn.silu(h @ w_gate) * (h @ w_up)) @ w_down


def fourier_mix(u):
    b, l, _ = u.shape
    uf = u.astype(jnp.float32).reshape(b, l, FOURIER_GROUPS, FOURIER_GROUP_DIM)
    y = jnp.fft.fft2(uf, axes=(1, 3), norm="ortho").real
    return y.reshape(b, l, D_FOURIER).astype(u.dtype)


def depthwise_conv_centred(x, w, bias):
    out = lax.conv_general_dilated(
        x, w[:, None, :].astype(x.dtype), window_strides=(1,),
        padding=[((D_CONV - 1) // 2, D_CONV // 2)],
        dimension_numbers=("NWC", "WIO", "NWC"),
        feature_group_count=x.shape[-1])
    return out + bias.astype(x.dtype)


def ssd_chunked(x, dt, A, B, C):
    b, l, g, k, p = x.shape
    n = B.shape[-1]
    q = SSD_CHUNK
    c = l // q
    xs = (x * dt[..., None]).reshape(b, c, q, g, k, p)
    a_cum = jnp.cumsum((dt * A).reshape(b, c, q, g, k), axis=2)
    Bc = B.reshape(b, c, q, g, n)
    Cc = C.reshape(b, c, q, g, n)
    lower = jnp.tril(jnp.ones((q, q), dtype=bool))[:, :, None, None]
    seg = a_cum[:, :, :, None] - a_cum[:, :, None, :]
    decay = jnp.exp(jnp.where(lower, seg, -jnp.inf))
    cb = jnp.einsum("bctgn,bcsgn->bctsg", Cc, Bc)
    y_diag = jnp.einsum("bctsg,bctsgk,bcsgkp->bctgkp", cb, decay, xs)
    decay_to_end = jnp.exp(a_cum[:, :, -1:] - a_cum)
    states = jnp.einsum("bcsgn,bcsgk,bcsgkp->bcgkpn", Bc, decay_to_end, xs)
    chunk_decay = jnp.exp(a_cum[:, :, -1])

    def step(h, inp):
        s, d = inp
        return h * d[..., None, None] + s, h

    h0 = jnp.zeros((b, g, k, p, n), xs.dtype)
    _, prev = lax.scan(step, h0, (jnp.moveaxis(states, 1, 0), jnp.moveaxis(chunk_decay, 1, 0)))
    prev = jnp.moveaxis(prev, 0, 1)
    y_off = jnp.einsum("bctgn,bcgkpn,bctgk->bctgkp", Cc, prev, jnp.exp(a_cum))
    return (y_diag + y_off).reshape(b, l, g, k, p)


def ssd_mixer(z, xbc, dt_raw, conv_w, conv_b, A_log, dt_bias, D_skip, gnorm_w):
    b, l, _ = z.shape
    G, K, P, N = SSD_GROUPS, SSD_HEADS_PER_GROUP, SSD_HEAD_DIM, D_STATE
    xbc = jax.nn.silu(depthwise_conv_centred(xbc, conv_w, conv_b)).astype(jnp.float32)
    xs = xbc[..., :D_SSD].reshape(b, l, G, K, P)
    Bm = xbc[..., D_SSD:D_SSD + G * N].reshape(b, l, G, N)
    Cm = xbc[..., D_SSD + G * N:].reshape(b, l, G, N)
    dt = jax.nn.softplus(dt_raw.astype(jnp.float32).reshape(b, l, 2, G, K)
                         + dt_bias.astype(jnp.float32).reshape(2, G, K))
    A = -jnp.exp(A_log.astype(jnp.float32)).reshape(2, G, K)
    flip = lambda t: jnp.flip(t, axis=1)
    y_fwd = ssd_chunked(xs, dt[:, :, 0], A[0], Bm, Cm)
    y_bwd = flip(ssd_chunked(flip(xs), flip(dt[:, :, 1]), A[1], flip(Bm), flip(Cm)))
    y = y_fwd + y_bwd + D_skip.astype(jnp.float32).reshape(G, K)[..., None] * xs
    y = y.reshape(b, l, D_SSD) * jax.nn.silu(z.astype(jnp.float32))
    yg = y.reshape(b, l, G, D_SSD // G)
    yg = yg * lax.rsqrt(jnp.mean(yg * yg, axis=-1, keepdims=True) + EPS)
    y = yg.reshape(b, l, D_SSD) * gnorm_w.astype(jnp.float32)
    return y.astype(z.dtype)


def fourier_ssd_layer(h, in_proj, conv_w, conv_b, A_log, dt_bias, D_skip, gnorm_w, out_proj):
    proj = h @ in_proj
    o1 = D_FOURIER
    o2 = o1 + D_SSD
    o3 = o2 + CONV_CH
    y_f = fourier_mix(proj[..., :o1])
    y_s = ssd_mixer(proj[..., o1:o2], proj[..., o2:o3], proj[..., o3:],
                    conv_w, conv_b, A_log, dt_bias, D_skip, gnorm_w)
    return jnp.concatenate([y_f, y_s], axis=-1) @ out_proj


def axial_rope_tables(l):
    rows = l // GRID_W
    row_idx = jnp.repeat(jnp.arange(rows), GRID_W)
    col_idx = jnp.tile(jnp.arange(GRID_W), rows)
    inv_freq = ROPE_THETA ** (-jnp.arange(0, ROPE_HALF, 2, dtype=jnp.float32) / ROPE_HALF)
    ang = jnp.stack([row_idx, col_idx], 0).astype(jnp.float32)[..., None] * inv_freq
    ang = jnp.moveaxis(ang, 0, 1)[:, None]
    return jnp.cos(ang), jnp.sin(ang)


def apply_axial_rope(x, cos, sin):
    xr = x.reshape(*x.shape[:-1], 2, 2, ROPE_HALF // 2)
    x1 = xr[..., 0, :]
    x2 = xr[..., 1, :]
    out = jnp.stack([x1 * cos - x2 * sin, x2 * cos + x1 * sin], axis=-2)
    return out.reshape(x.shape)


def gqa_axial_attention(h, w_qkv, q_norm, k_norm, w_o):
    b, l, _ = h.shape
    qkv = h @ w_qkv
    nq = N_HEADS * HEAD_DIM
    nk = N_KV_HEADS * HEAD_DIM
    q = rmsnorm(qkv[..., :nq].reshape(b, l, N_HEADS, HEAD_DIM), q_norm).astype(jnp.float32)
    k = rmsnorm(qkv[..., nq:nq + nk].reshape(b, l, N_KV_HEADS, HEAD_DIM), k_norm).astype(jnp.float32)
    v = qkv[..., nq + nk:].reshape(b, l, N_KV_HEADS, HEAD_DIM)
    cos, sin = axial_rope_tables(l)
    q = (apply_axial_rope(q, cos, sin) * HEAD_DIM ** -0.5).astype(h.dtype)
    k = apply_axial_rope(k, cos, sin).astype(h.dtype)
    q = q.reshape(b, l // Q_BLOCK, Q_BLOCK, N_KV_HEADS, KV_GROUP, HEAD_DIM)
    q = jnp.moveaxis(q, 1, 0)

    def attend_block(qb):
        s = jnp.einsum("bqkgd,bskd->bkgqs", qb, k).astype(jnp.float32)
        p = jax.nn.softmax(s, axis=-1)
        return jnp.einsum("bkgqs,bskd->bqkgd", p.astype(v.dtype), v)

    o = lax.map(attend_block, q)
    o = jnp.moveaxis(o, 0, 1).reshape(b, l, N_HEADS * HEAD_DIM)
    return o @ w_o


def setup_inputs(seed: int = 0) -> dict:
    key = jax.random.key(seed)
    ks = jax.random.split(key, 20)
    nrm = jax.random.normal
    dt = jnp.exp(jax.random.uniform(ks[10], (N_EVEN, 2, SSD_HEADS))
                 * (np.log(0.1) - np.log(0.001)) + np.log(0.001))
    return {
        "x": nrm(ks[0], (BATCH, SEQ, D_MODEL)),
        "ffn_norm": 1.0 + 0.02 * nrm(ks[1], (DEPTH, 2, D_MODEL)),
        "ffn_w_gate": nrm(ks[2], (DEPTH, 2, D_MODEL, D_FF)) * D_MODEL ** -0.5,
        "ffn_w_up": nrm(ks[3], (DEPTH, 2, D_MODEL, D_FF)) * D_MODEL ** -0.5,
        "ffn_w_down": nrm(ks[4], (DEPTH, 2, D_FF, D_MODEL)) * D_FF ** -0.5,
        "mix_norm": 1.0 + 0.02 * nrm(ks[5], (DEPTH, D_MODEL)),
        "hyb_in_proj": nrm(ks[6], (N_EVEN, D_MODEL, IN_COLS)) * D_MODEL ** -0.5,
        "ssd_conv_w": nrm(ks[7], (N_EVEN, D_CONV, CONV_CH)) * D_CONV ** -0.5,
        "ssd_conv_b": 0.02 * nrm(ks[8], (N_EVEN, CONV_CH)),
        "ssd_A_log": jnp.log(jax.random.uniform(ks[9], (N_EVEN, 2, SSD_HEADS), minval=1.0, maxval=16.0)),
        "ssd_dt_bias": dt + jnp.log(-jnp.expm1(-dt)),
        "ssd_D": 1.0 + 0.02 * nrm(ks[11], (N_EVEN, SSD_HEADS)),
        "ssd_gnorm": 1.0 + 0.02 * nrm(ks[12], (N_EVEN, D_SSD)),
        "hyb_out_proj": nrm(ks[13], (N_EVEN, MIX_WIDTH, D_MODEL)) * MIX_WIDTH ** -0.5,
        "attn_w_qkv": nrm(ks[14], (N_ODD, D_MODEL, QKV_COLS)) * D_MODEL ** -0.5,
        "attn_q_norm": 1.0 + 0.02 * nrm(ks[15], (N_ODD, HEAD_DIM)),
        "attn_k_norm": 1.0 + 0.02 * nrm(ks[16], (N_ODD, HEAD_DIM)),
        "attn_w_o": nrm(ks[17], (N_ODD, N_HEADS * HEAD_DIM, D_MODEL)) * (N_HEADS * HEAD_DIM) ** -0.5,
        "final_norm": 1.0 + 0.02 * nrm(ks[18], (D_MODEL,)),
    }


def reference(x, ffn_norm, ffn_w_gate, ffn_w_up, ffn_w_down, mix_norm, hyb_in_proj,
              ssd_conv_w, ssd_conv_b, ssd_A_log, ssd_dt_bias, ssd_D, ssd_gnorm,
              hyb_out_proj, attn_w_qkv, attn_q_norm, attn_k_norm, attn_w_o, final_norm):
    for i in range(DEPTH):
        j = i // 2
        x = x + 0.5 * swiglu_ffn(rmsnorm(x, ffn_norm[i, 0]),
                                 ffn_w_gate[i, 0], ffn_w_up[i, 0], ffn_w_down[i, 0])
        hn = rmsnorm(x, mix_norm[i])
        if i % 2 == 0:
            x = x + fourier_ssd_layer(hn, hyb_in_proj[j], ssd_conv_w[j], ssd_conv_b[j],
                                      ssd_A_log[j], ssd_dt_bias[j], ssd_D[j], ssd_gnorm[j],
                                      hyb_out_proj[j])
        else:
            x = x + gqa_axial_attention(hn, attn_w_qkv[j], attn_q_norm[j], attn_k_norm[j], attn_w_o[j])
        x = x + 0.5 * swiglu_ffn(rmsnorm(x, ffn_norm[i, 1]),
                                 ffn_w_gate[i, 1], ffn_w_up[i, 1], ffn_w_down[i, 1])
    return rmsnorm(x, final_norm)
```

```python
import contextlib
import numpy as np
import ml_dtypes
import concourse.bass as bass
import concourse.mybir as mybir
from concourse.bass_utils import run_bass_kernel_spmd

F32 = mybir.dt.float32
BF16 = mybir.dt.bfloat16
AF = mybir.ActivationFunctionType
ALU = mybir.AluOpType
AX = mybir.AxisListType

NCORE = 8
REPLICATE = True


def nsh():
    return 1 if REPLICATE else NCORE
D = 4096
L = 2048
DFF = 11008
KC = D // 128
FC = DFF // 128
TT = 512
NTT = L // TT
EPS = 1e-6
FFN_SH = 44032
NCF = 128 * 7 + 8
C_MINC, C_MDEC, C_SGT, C_SLT, C_PIT = 264, 392, 520, 648, 776
M0_W = [("inw", 65536), ("dtw", 768), ("outw", 32768), ("dftc", 4096), ("dfts", 4096)]
M1_W = [("qkvw", 24576), ("wow", 16384)]
M0_C = [("mn0", [128, 32], F32), ("cw", [128, 64, 5], F32), ("cb", [128, 64], F32), ("dtb", [128, 192], F32),
        ("alog", [128, 192], F32), ("dsk", [128, 96], F32), ("gnw", [128, 48], F32), ("cdsd", [128, 2, 512], BF16)]
M1_C = [("mn1", [128, 32], F32), ("qkn", [128, 2], F32), ("cos", [128, 2048], F32), ("sin", [128, 2048], F32)]

SAME_ENG_SYNC = True


class Sched:
    ENGS = ("pe", "act", "dve", "pool", "sp")
    KDMA = 16
    PSUM_KEYS = {"pj", "pq_ps", "PC", "P3", "PA", "PB", "tp", "tpo", "ssq", "px", "o_ps", "r_ps", "s_ps",
                 "g_ps", "u_ps", "y_ps", "ssq_ps"}
    EPOCH = 3000

    def __init__(self, nc):
        self.nc = nc
        self.ops = []
        self.last_writer = {}
        self.readers_c = {}
        self.readers_d = {}
        self.n_cc = 0

    def add(self, eng, fn, reads=(), writes=(), kind="c"):
        i = len(self.ops)
        deps = set()
        reads = tuple(reads) + ("__phase__",)
        for k in reads:
            w = self.last_writer.get(k)
            if w is not None:
                deps.add(w)
            if (k if isinstance(k, str) else k[0]) in self.PSUM_KEYS:
                for re_, ri in self.readers_c.get(k, {}).items():
                    if re_ != eng:
                        deps.add(ri)
        for k in writes:
            w = self.last_writer.get(k)
            if w is not None:
                deps.add(w)
            deps.update(self.readers_c.get(k, {}).values())
            deps.update(self.readers_d.get(k, ()))
        for k in reads:
            if kind == "c":
                self.readers_c.setdefault(k, {})[eng] = i
            else:
                if k != "__phase__" or True:
                    self.readers_d.setdefault(k, []).append(i)
        for k in writes:
            self.last_writer[k] = i
            self.readers_c[k] = {}
            self.readers_d[k] = []
        cc = None
        if kind == "cc":
            cc = self.n_cc
            self.n_cc += 1
        self.ops.append(dict(eng=eng, fn=fn, deps=deps, kind=kind, cc=cc))
        return i

    def pe(self, fn, reads=(), writes=()):
        return self.add("pe", fn, reads, writes)

    def act(self, fn, reads=(), writes=()):
        return self.add("act", fn, reads, writes)

    def dve(self, fn, reads=(), writes=()):
        return self.add("dve", fn, reads, writes)

    def pool(self, fn, reads=(), writes=()):
        return self.add("pool", fn, reads, writes)

    def dma(self, out, in_, reads=(), writes=(), q="sp"):
        return self.add(q, lambda e: e.dma_start(out=out, in_=in_), reads, writes, kind="d")

    def barrier(self, scratch):
        self.add("dve", lambda e: e.memset(scratch, 0.0), reads=(), writes=("__phase__",))

    def emit(self):
        nc = self.nc
        ops = self.ops
        need = [False] * len(ops)
        for o in ops:
            for d in o["deps"]:
                if ops[d]["eng"] == "pe" and o["eng"] == "pe" and ops[d]["kind"] == "c" and o["kind"] == "c":
                    continue
                if (not SAME_ENG_SYNC) and ops[d]["eng"] == o["eng"] and ops[d]["kind"] == "c" and o["kind"] == "c":
                    continue
                need[d] = True
        cnt = {e: 0 for e in self.ENGS}
        dcnt = {e: 0 for e in self.ENGS}
        dhist = {e: [] for e in self.ENGS}
        by_eng = {e: [] for e in self.ENGS}
        for i, o in enumerate(ops):
            e = o["eng"]
            by_eng[e].append(i)
            o["ticket"] = None
            o["prev"] = None
            if o["kind"] == "c":
                if need[i]:
                    cnt[e] += 1
                    o["ticket"] = (("c", e, (cnt[e] - 1) // self.EPOCH), (cnt[e] - 1) % self.EPOCH + 1, 1)
            elif o["kind"] == "d":
                j = dcnt[e]
                dcnt[e] += 1
                o["ticket"] = (("d", e, j % self.KDMA), 16 * (j // self.KDMA + 1), 16)
                if j >= self.KDMA:
                    o["prev"] = dhist[e][j - self.KDMA]
                dhist[e].append(i)
            else:
                o["ticket"] = (("cc", o["cc"]), 1, 1)
        with contextlib.ExitStack() as st:
            sems = {}

            def sem(key):
                if key not in sems:
                    sems[key] = st.enter_context(nc.semaphore("s_" + "_".join(str(x) for x in key)))
                return sems[key]

            for e in self.ENGS:
                for ep in range((cnt[e] + self.EPOCH - 1) // self.EPOCH + 1):
                    sem(("c", e, ep))
                if dcnt[e]:
                    for s in range(self.KDMA):
                        sem(("d", e, s))
            for c in range(self.n_cc):
                sem(("cc", c))
            block = st.enter_context(nc.Block())

            def run(engname, eng):
                waited = {}
                wep = {}
                for i in by_eng[engname]:
                    o = ops[i]
                    waits = {}
                    dl = list(o["deps"])
                    if o["prev"] is not None:
                        dl.append(o["prev"])
                    for d in dl:
                        od = ops[d]
                        if od["kind"] == "c" and o["kind"] == "c" and od["eng"] == engname:
                            if engname == "pe" or not SAME_ENG_SYNC:
                                continue
                        key, val, _ = od["ticket"]
                        if key[0] == "c" and wep.get(key[1], -1) > key[2]:
                            continue
                        if waited.get(key, 0) >= val:
                            continue
                        if waits.get(key, 0) < val:
                            waits[key] = val
                    for key, val in waits.items():
                        eng.wait_ge(sem(key), val)
                        waited[key] = val
                        if key[0] == "c":
                            wep[key[1]] = max(wep.get(key[1], -1), key[2])
                    ins = o["fn"](eng)
                    if o["ticket"] is not None:
                        key, val, inc = o["ticket"]
                        ins.then_inc(sem(key), inc)
                for i in by_eng[engname]:
                    o = ops[i]
                    if o["kind"] in ("d", "cc"):
                        key, val, _ = o["ticket"]
                        if waited.get(key, 0) < val:
                            waited[key] = val
                final = {}
                for i in by_eng[engname]:
                    o = ops[i]
                    if o["kind"] in ("d", "cc"):
                        key, val, _ = o["ticket"]
                        final[key] = max(final.get(key, 0), val)
                for key, val in final.items():
                    eng.wait_ge(sem(key), val)

            @block.tensor
            def _(eng):
                run("pe", eng)

            @block.scalar
            def _(eng):
                run("act", eng)

            @block.vector
            def _(eng):
                run("dve", eng)

            @block.gpsimd
            def _(eng):
                run("pool", eng)

            @block.sync
            def _(eng):
                run("sp", eng)


class Prog:
    def __init__(self, cfg):
        self.cfg = cfg
        self.nc = bass.Bass("TRN2", target_bir_lowering=False)
        self.S = Sched(self.nc)
        self.ins = {}
        self._uid = 0

    def uid(self, p):
        self._uid += 1
        return f"{p}{self._uid}"

    def ext_in(self, name, shape, dt=F32):
        self.ins[name] = (shape, dt)
        return self.nc.dram_tensor(name, list(shape), dt, kind="ExternalInput").ap()

    def scratch(self, name, shape, dt):
        return self.nc.dram_tensor(name, list(shape), dt).ap()

    @contextlib.contextmanager
    def phase(self):
        with contextlib.ExitStack() as st:
            def sb(shape, dt, name=None):
                return st.enter_context(self.nc.sbuf_tensor(name or self.uid("t"), list(shape), dt))

            def ps(shape, dt=F32, name=None):
                return st.enter_context(self.nc.psum_tensor(name or self.uid("p"), list(shape), dt))
            yield sb, ps
            self.S.barrier(self.bar_scratch[:])

    def gather_weight(self, name, shard_ap, ncols, stage, stage_bf, piece):
        S = self.S
        NS = nsh()
        gath = self.scratch(name + "_g", [128 * NS, ncols], BF16)
        bounce = gath if NS == 1 else self.scratch(name + "_b", [128, ncols], BF16)
        wkey = ("gath", name) if NS == 1 else ("bounce", name)
        npieces = (ncols + piece - 1) // piece
        for j in range(npieces):
            c0 = j * piece
            w = min(piece, ncols - c0)
            sl = self._gw_i % len(stage)
            self._gw_i += 1
            sf, sbf = stage[sl], stage_bf[sl]
            S.dma(sf[:, :w], shard_ap[:, c0:c0 + w], writes=[("gwf", sl)], q="sp")
            S.dve(lambda e, sf=sf, sbf=sbf, w=w: e.tensor_copy(out=sbf[:, :w], in_=sf[:, :w]),
                  reads=[("gwf", sl)], writes=[("gwb", sl)])
            S.dma(bounce[:, c0:c0 + w], sbf[:, :w], reads=[("gwb", sl)], writes=[(wkey[0], name, j)], q="pool")
        if NS == 1:
            S.dve(lambda e: e.memset(self.bar_scratch[:, 0:4], 0.0),
                  reads=[("gath", name, j) for j in range(npieces)], writes=[("gath", name)])
        else:
            S.dve(lambda e: e.memset(self.bar_scratch[:, 0:4], 0.0),
                  reads=[("bounce", name, j) for j in range(npieces)], writes=[("bounce", name)])
            S.add("pool", lambda e: e.collective_compute("AllGather", ALU.bypass,
                                                         replica_groups=[list(range(NCORE))],
                                                         ins=[bounce.opt()], outs=[gath.opt()]),
                  reads=[("bounce", name)], writes=[("gath", name)], kind="cc")
        return gath

    def build(self):
        cfg = self.cfg
        nc, S = self.nc, self.S
        stages = cfg["stages"]
        ffn_list = [(int(t[3]), int(t[4])) for t in stages if t.startswith("ffn")]
        x_in = self.ext_in("x", [L, D])
        cf = self.ext_in("cf32", [128, NCF])
        ffn_w = {}
        ffn_nw = {}
        for (i, s) in ffn_list:
            for nm in ("g", "u", "d"):
                ffn_w[(i, s, nm)] = self.ext_in(f"ffn{i}{s}{nm}", [128, FFN_SH * 8 // nsh()])
            ffn_nw[(i, s)] = self.ext_in(f"ffn{i}{s}n", [128, KC])
        mw = {}
        if "mix0" in stages:
            for nm, ncols in M0_W:
                mw[nm] = self.ext_in(nm, [128, ncols * 8 // nsh()])
            for nm, shp, dt in M0_C:
                mw[nm] = self.ext_in(nm, shp, dt)
        if "mix1" in stages:
            for nm, ncols in M1_W:
                mw[nm] = self.ext_in(nm, [128, ncols * 8 // nsh()])
            for nm, shp, dt in M1_C:
                mw[nm] = self.ext_in(nm, shp, dt)
        fin_w = self.ext_in("finw", [128, D]) if cfg.get("final_norm", True) else None
        out = nc.dram_tensor("out", [L, D], F32, kind="ExternalOutput").ap()
        xT = self.scratch("xT", [D, L], F32)
        xTv = xT.rearrange("(k p) l -> p k l", p=128)

        with contextlib.ExitStack() as gst:
            self.bar_scratch = gst.enter_context(nc.sbuf_tensor("bar_scr", [128, 8], F32))
            cft = gst.enter_context(nc.sbuf_tensor("cft", [128, NCF], F32))
            S.dma(cft[:], cf[:, :], writes=["cft"])
            self.cft = cft
            ident = cft[:, 0:128]
            ones = cft[:, 128:256]
            self.ident, self.ones = ident, ones
            self.eps_ap = cft[:, 256:257]
            self.one_col = cft[:, 128:129]

            gathered = {}
            self._gw_i = 0
            with self.phase() as (sb, ps):
                PIECE = 2752
                stage = [sb([128, PIECE], F32) for _ in range(3)]
                stage_bf = [sb([128, PIECE], BF16) for _ in range(3)]
                for t in stages:
                    if t.startswith("ffn"):
                        i, s = int(t[3]), int(t[4])
                        for nm in ("g", "u", "d"):
                            gathered[(i, s, nm)] = self.gather_weight(f"w{i}{s}{nm}", ffn_w[(i, s, nm)], FFN_SH * 8 // nsh(),
                                                                      stage, stage_bf, PIECE)
                    elif t == "mix0":
                        for nm, ncols in M0_W:
                            gathered[nm] = self.gather_weight(nm, mw[nm], ncols * 8 // nsh(), stage, stage_bf, PIECE)
                    elif t == "mix1":
                        for nm, ncols in M1_W:
                            gathered[nm] = self.gather_weight(nm, mw[nm], ncols * 8 // nsh(), stage, stage_bf, PIECE)
                xrow = [sb([128, D], F32) for _ in range(2)]
                xst = [sb([128, KC, 128], F32) for _ in range(2)]
                tp = [ps([128, 512]) for _ in range(2)]
                for m in range(L // 128):
                    b = m % 2
                    S.dma(xrow[b][:], x_in[m * 128:(m + 1) * 128, :], writes=[("xrow", b)])
                    for k4 in range(KC // 4):
                        pb = k4 % 2
                        for j in range(4):
                            kc = k4 * 4 + j
                            S.pe(lambda e, b=b, pb=pb, j=j, kc=kc: e.transpose(
                                out=tp[pb][:, j * 128:(j + 1) * 128], in_=xrow[b][:, kc * 128:(kc + 1) * 128],
                                identity=ident), reads=[("xrow", b), "cft"], writes=[("tp", pb)])
                        S.act(lambda e, b=b, pb=pb, k4=k4: e.activation(
                            out=xst[b][:, k4 * 4:(k4 + 1) * 4, :].rearrange("p k t -> p (k t)"), in_=tp[pb][:],
                            func=AF.Copy), reads=[("tp", pb)], writes=[("xst", b)])
                    S.dma(xTv[:, :, m * 128:(m + 1) * 128], xst[b][:], reads=[("xst", b)],
                          writes=[("xT", m // 4)])

            for t in stages:
                if t.startswith("ffn"):
                    i, s = int(t[3]), int(t[4])
                    self.ffn_phase(gathered[(i, s, "g")], gathered[(i, s, "u")], gathered[(i, s, "d")],
                                   ffn_nw[(i, s)], (f"w{i}{s}g", f"w{i}{s}u", f"w{i}{s}d"), xTv, ones)
                elif t == "mix0":
                    self.mix0(gathered, mw, xTv)
                elif t == "mix1":
                    self.mix1(gathered, mw, xTv)

            with self.phase() as (sb, ps):
                xko = [sb([128, KC, 128], F32) for _ in range(2)]
                orow = [sb([128, D], F32) for _ in range(2)]
                tpo = [ps([128, 512]) for _ in range(2)]
                junk = sb([128, D], F32)
                ssq = sb([128, 2], F32)
                rstd = sb([128, 2], F32)
                if fin_w is not None:
                    fw = sb([128, D], F32)
                    S.dma(fw[:], fin_w[:, :], writes=["fw"])
                for m in range(L // 128):
                    b = m % 2
                    S.dma(xko[b][:], xTv[:, :, m * 128:(m + 1) * 128], reads=[("xT", m // 4)], writes=[("xk", b)])
                    for k4 in range(KC // 4):
                        pb = k4 % 2
                        for j in range(4):
                            kc = k4 * 4 + j
                            S.pe(lambda e, b=b, pb=pb, j=j, kc=kc: e.transpose(
                                out=tpo[pb][:, j * 128:(j + 1) * 128], in_=xko[b][:, kc, :], identity=ident),
                                reads=[("xk", b), "cft"], writes=[("tpo", pb)])
                        S.act(lambda e, b=b, pb=pb, k4=k4: e.activation(
                            out=orow[b][:, k4 * 512:(k4 + 1) * 512], in_=tpo[pb][:], func=AF.Copy),
                            reads=[("tpo", pb)], writes=[("orow", b)])
                    if fin_w is not None:
                        S.act(lambda e, b=b: e.activation(out=junk[:], in_=orow[b][:], func=AF.Square,
                                                          accum_out=ssq[:, b:b + 1]),
                              reads=[("orow", b)], writes=["junk", ("ssq", b)])
                        S.act(lambda e, b=b: e.activation(out=rstd[:, b:b + 1], in_=ssq[:, b:b + 1], func=AF.Sqrt,
                                                          scale=1.0 / D, bias=self.eps_ap),
                              reads=[("ssq", b)], writes=[("rstd", b)])
                        S.dve(lambda e, b=b: e.reciprocal(out=rstd[:, b:b + 1], in_=rstd[:, b:b + 1]),
                              reads=[("rstd", b)], writes=[("rstd", b)])
                        S.dve(lambda e, b=b: e.scalar_tensor_tensor(out=orow[b][:], in0=orow[b][:],
                                                                    scalar=rstd[:, b:b + 1], in1=fw[:],
                                                                    op0=ALU.mult, op1=ALU.mult),
                              reads=[("orow", b), ("rstd", b), "fw"], writes=[("orow", b)])
                    S.dma(out[m * 128:(m + 1) * 128, :], orow[b][:], reads=[("orow", b)], writes=[("out", m)])
            S.emit()
        return nc

    def ffn_phase(self, gW, uW, dW, nw_ap, wnames, xTv, ones):
        S = self.S
        gv = gW.rearrange("r j -> (r j)").rearrange("(f p k) -> f p k", p=128, k=D)
        uv = uW.rearrange("r j -> (r j)").rearrange("(f p k) -> f p k", p=128, k=D)
        dv = dW.rearrange("r j -> (r j)").rearrange("(d p k) -> d p k", p=128, k=DFF)
        with self.phase() as (sb, ps):
            hT = sb([128, KC, TT], BF16)
            aT = sb([128, FC, TT], BF16)
            nw = sb([128, KC], F32)
            xk = [sb([128, TT], F32) for _ in range(3)]
            sq = [sb([128, TT], F32) for _ in range(2)]
            rstd = sb([128, TT], F32)
            wg = [sb([128, D], BF16) for _ in range(2)]
            wu = [sb([128, D], BF16) for _ in range(2)]
            wd = [sb([128, DFF // 2], BF16) for _ in range(3)]
            sg = [sb([128, TT], F32) for _ in range(2)]
            xo = [sb([128, TT], F32) for _ in range(2)]
            ssq_ps = ps([128, TT])
            g_ps = [ps([128, TT]) for _ in range(2)]
            u_ps = [ps([128, TT]) for _ in range(2)]
            o_ps = [ps([128, TT]) for _ in range(2)]
            S.dma(nw[:], nw_ap[:, :], writes=["nw"])
            ld = 0
            wdi = 0
            for tt in range(NTT):
                tsl = slice(tt * TT, (tt + 1) * TT)
                for kc in range(KC):
                    b = ld % 3
                    ld += 1
                    q = kc % 2
                    S.dma(xk[b][:], xTv[:, kc, tsl], reads=[("xT", tt)], writes=[("xk", b)])
                    S.act(lambda e, b=b, q=q: e.activation(out=sq[q][:], in_=xk[b][:], func=AF.Square),
                          reads=[("xk", b)], writes=[("sq", q)])
                    S.pe(lambda e, q=q, kc=kc: e.matmul(ssq_ps[:], lhsT=ones, rhs=sq[q][:],
                                                       start=(kc == 0), stop=(kc == KC - 1)),
                         reads=[("sq", q), "cft"], writes=["ssq_ps"])
                S.act(lambda e: e.activation(out=rstd[:], in_=ssq_ps[:], func=AF.Sqrt, scale=1.0 / D,
                                             bias=self.eps_ap), reads=["ssq_ps"], writes=["rstd"])
                S.dve(lambda e: e.reciprocal(out=rstd[:], in_=rstd[:]), reads=["rstd"], writes=["rstd"])
                for kc in range(KC):
                    b = ld % 3
                    ld += 1
                    S.dma(xk[b][:], xTv[:, kc, tsl], reads=[("xT", tt)], writes=[("xk", b)])
                    S.dve(lambda e, b=b, kc=kc: e.scalar_tensor_tensor(
                        out=hT[:, kc, :], in0=xk[b][:], scalar=nw[:, kc:kc + 1], in1=rstd[:],
                        op0=ALU.mult, op1=ALU.mult), reads=[("xk", b), "nw", "rstd"], writes=["hT"])
                for fc in range(FC):
                    wb = fc % 2
                    S.dma(wg[wb][:], gv[fc], reads=[("gath", wnames[0])], writes=[("wg", wb)])
                    S.dma(wu[wb][:], uv[fc], reads=[("gath", wnames[1])], writes=[("wu", wb)])
                    for kc in range(KC):
                        S.pe(lambda e, wb=wb, kc=kc: e.matmul(g_ps[wb][:], lhsT=wg[wb][:, kc * 128:(kc + 1) * 128],
                                                            rhs=hT[:, kc, :], start=(kc == 0), stop=(kc == KC - 1)),
                             reads=[("wg", wb), "hT"], writes=[("g_ps", wb)])
                    for kc in range(KC):
                        S.pe(lambda e, wb=wb, kc=kc: e.matmul(u_ps[wb][:], lhsT=wu[wb][:, kc * 128:(kc + 1) * 128],
                                                            rhs=hT[:, kc, :], start=(kc == 0), stop=(kc == KC - 1)),
                             reads=[("wu", wb), "hT"], writes=[("u_ps", wb)])
                    S.act(lambda e, wb=wb: e.activation(out=sg[wb][:], in_=g_ps[wb][:], func=AF.Silu),
                          reads=[("g_ps", wb)], writes=[("sg", wb)])
                    S.dve(lambda e, wb=wb, fc=fc: e.tensor_tensor(out=aT[:, fc, :], in0=sg[wb][:], in1=u_ps[wb][:],
                                                                  op=ALU.mult),
                          reads=[("sg", wb), ("u_ps", wb)], writes=["aT"])
                for dc in range(KC):
                    wb = dc % 2
                    b = ld % 3
                    ld += 1
                    S.dma(xk[b][:], xTv[:, dc, tsl], reads=[("xT", tt)], writes=[("xk", b)])
                    for hf in range(2):
                        db = wdi % 3
                        wdi += 1
                        S.dma(wd[db][:], dv[dc][:, hf * (DFF // 2):(hf + 1) * (DFF // 2)],
                              reads=[("gath", wnames[2])], writes=[("wd", db)])
                        for f2 in range(FC // 2):
                            fc = hf * (FC // 2) + f2
                            S.pe(lambda e, wb=wb, db=db, fc=fc, f2=f2: e.matmul(
                                o_ps[wb][:], lhsT=wd[db][:, f2 * 128:(f2 + 1) * 128],
                                rhs=aT[:, fc, :], start=(fc == 0), stop=(fc == FC - 1)),
                                reads=[("wd", db), "aT"], writes=[("o_ps", wb)])
                    S.dve(lambda e, wb=wb, b=b: e.scalar_tensor_tensor(
                        out=xo[wb][:], in0=o_ps[wb][:], scalar=0.5, in1=xk[b][:], op0=ALU.mult, op1=ALU.add),
                        reads=[("o_ps", wb), ("xk", b)], writes=[("xo", wb)])
                    S.dma(xTv[:, dc, tsl], xo[wb][:], reads=[("xo", wb)], writes=[("xTw", tt, dc)], q="pool")
                S.dve(lambda e: e.memset(self.bar_scratch[:, 4:8], 0.0),
                      reads=[("xTw", tt, dc) for dc in range(KC)], writes=[("xT", tt)])

    def norm_setup(self, sb, ps, nw_ap):
        S = self.S
        r = dict(nw=sb([128, KC], F32), xk=[sb([128, TT], F32) for _ in range(2)],
                 sq=[sb([128, TT], F32) for _ in range(2)], rstd=sb([128, TT], F32), ssq=ps([128, TT]),
                 u=self.uid("nrm"), ld=0)
        S.dma(r["nw"][:], nw_ap[:, :], writes=[r["u"] + "nw"])
        return r

    def norm_tile(self, r, xTv, tt, out_fn, out_key):
        S = self.S
        u = r["u"]
        tsl = slice(tt * TT, (tt + 1) * TT)
        xk, sq, rstd, ssq, nw = r["xk"], r["sq"], r["rstd"], r["ssq"], r["nw"]
        for kc in range(KC):
            b = r["ld"] % 2
            r["ld"] += 1
            q = kc % 2
            S.dma(xk[b][:], xTv[:, kc, tsl], reads=[("xT", tt)], writes=[(u, "xk", b)])
            S.act(lambda e, b=b, q=q: e.activation(out=sq[q][:], in_=xk[b][:], func=AF.Square),
                  reads=[(u, "xk", b)], writes=[(u, "sq", q)])
            S.pe(lambda e, q=q, kc=kc: e.matmul(ssq[:], lhsT=self.ones, rhs=sq[q][:],
                                               start=(kc == 0), stop=(kc == KC - 1)),
                 reads=[(u, "sq", q), "cft"], writes=[(u, "ssq")])
        S.act(lambda e: e.activation(out=rstd[:], in_=ssq[:], func=AF.Sqrt, scale=1.0 / D, bias=self.eps_ap),
              reads=[(u, "ssq"), "cft"], writes=[(u, "rstd")])
        S.dve(lambda e: e.reciprocal(out=rstd[:], in_=rstd[:]), reads=[(u, "rstd")], writes=[(u, "rstd")])
        for kc in range(KC):
            b = r["ld"] % 2
            r["ld"] += 1
            S.dma(xk[b][:], xTv[:, kc, tsl], reads=[("xT", tt)], writes=[(u, "xk", b)])
            S.dve(lambda e, b=b, kc=kc: e.scalar_tensor_tensor(
                out=out_fn(kc), in0=xk[b][:], scalar=nw[:, kc:kc + 1], in1=rstd[:],
                op0=ALU.mult, op1=ALU.mult), reads=[(u, "xk", b), u + "nw", (u, "rstd")], writes=[out_key])

    def proj_residual(self, inv, wv, gkey, Kc, xTv):
        S = self.S
        with self.phase() as (sb, ps):
            aT = sb([128, Kc, TT], BF16)
            wd = [sb([128, Kc * 128], BF16) for _ in range(2)]
            xk = [sb([128, TT], F32) for _ in range(2)]
            xo = [sb([128, TT], F32) for _ in range(2)]
            o_ps = [ps([128, TT]) for _ in range(2)]
            for tt in range(NTT):
                tsl = slice(tt * TT, (tt + 1) * TT)
                S.dma(aT[:], inv[:, :, tsl], writes=["aT"])
                for dc in range(KC):
                    wb = dc % 2
                    S.dma(wd[wb][:], wv[dc], reads=[("gath", gkey)], writes=[("wd", wb)])
                    S.dma(xk[wb][:], xTv[:, dc, tsl], reads=[("xT", tt)], writes=[("xk", wb)], q="pool")
                    for kc in range(Kc):
                        S.pe(lambda e, wb=wb, kc=kc: e.matmul(o_ps[wb][:], lhsT=wd[wb][:, kc * 128:(kc + 1) * 128],
                                                            rhs=aT[:, kc, :], start=(kc == 0), stop=(kc == Kc - 1)),
                             reads=[("wd", wb), "aT"], writes=[("o_ps", wb)])
                    S.dve(lambda e, wb=wb: e.tensor_tensor(out=xo[wb][:], in0=o_ps[wb][:], in1=xk[wb][:], op=ALU.add),
                          reads=[("o_ps", wb), ("xk", wb)], writes=[("xo", wb)])
                    S.dma(xTv[:, dc, tsl], xo[wb][:], reads=[("xo", wb)], writes=[("xTw", tt, dc)], q="pool")
                S.dve(lambda e: e.memset(self.bar_scratch[:, 4:8], 0.0),
                      reads=[("xTw", tt, dc) for dc in range(KC)], writes=[("xT", tt)])

    def mix1(self, G, mw, xTv):
        S = self.S
        cft = self.cft
        qkvw = G["qkvw"].rearrange("r j -> (r j)").rearrange("(n p k) -> n p k", p=128, k=D)
        wow = G["wow"].rearrange("r j -> (r j)").rearrange("(n p k) -> n p k", p=128, k=D)
        qkvT = self.scratch("qkvT", [48 * 128, L], BF16)
        attnT = self.scratch("attnT", [D, L], BF16)
        V_tm = self.scratch("V_tm", [L, 1024], BF16)
        PiT = cft[:, C_PIT:C_PIT + 128]
        with self.phase() as (sb, ps):
            hT = sb([128, KC, L], BF16)
            nr = self.norm_setup(sb, ps, mw["mn1"])
            qkn = sb([128, 2], F32)
            cos = sb([128, L], F32)
            sin = sb([128, L], F32)
            S.dma(qkn[:], mw["qkn"][:, :], writes=["qkn"])
            S.dma(cos[:], mw["cos"][:, :], writes=["cos"])
            S.dma(sin[:], mw["sin"][:, :], writes=["sin"])
            for tt in range(NTT):
                self.norm_tile(nr, xTv, tt, lambda kc, tt=tt: hT[:, kc, tt * TT:(tt + 1) * TT], "hT")
            w = [sb([128, D], BF16) for _ in range(2)]
            pj = [ps([128, TT]) for _ in range(5)]
            ssq = ps([128, TT])
            px = ps([128, TT])
            xs = [sb([128, TT], F32) for _ in range(2)]
            sq = [sb([128, TT], F32) for _ in range(2)]
            xw = [sb([128, TT], F32) for _ in range(2)]
            t1 = [sb([128, TT], F32) for _ in range(2)]
            t2 = [sb([128, TT], F32) for _ in range(2)]
            rs = [sb([128, TT], F32) for _ in range(2)]
            obt = [sb([128, TT], BF16) for _ in range(2)]
            it = 0
            vstg = [sb([128, 4, 128], BF16) for _ in range(2)]
            for hc in range(48):
                wb = hc % 2
                S.dma(w[wb][:], qkvw[hc], reads=[("gath", "qkvw")], writes=[("w", wb)])
                if hc >= 40:
                    kvh = hc - 40
                    for m4 in range(4):
                        pb = it % 5
                        b = it % 2
                        it += 1
                        for j in range(4):
                            m = m4 * 4 + j
                            for kc in range(KC):
                                S.pe(lambda e, wb=wb, kc=kc, pb=pb, j=j, m=m: e.matmul(
                                    pj[pb][:, j * 128:(j + 1) * 128], lhsT=hT[:, kc, m * 128:(m + 1) * 128],
                                    rhs=w[wb][:, kc * 128:(kc + 1) * 128], start=(kc == 0), stop=(kc == KC - 1)),
                                    reads=[("w", wb), "hT"], writes=[("pj", pb)])
                        S.act(lambda e, b=b, pb=pb: e.activation(out=vstg[b][:].rearrange("p a d -> p (a d)"),
                                                                 in_=pj[pb][:], func=AF.Copy),
                              reads=[("pj", pb)], writes=[("vstg", b)])
                        S.dma(V_tm[m4 * 512:(m4 + 1) * 512, kvh * 128:(kvh + 1) * 128].rearrange(
                            "(a p) d -> p a d", p=128), vstg[b][:], reads=[("vstg", b)], q="pool")
                    continue
                for tt in range(NTT):
                    tsl = slice(tt * TT, (tt + 1) * TT)
                    pb = it % 5
                    b = it % 2
                    it += 1
                    for kc in range(KC):
                        S.pe(lambda e, wb=wb, kc=kc, pb=pb, tsl=tsl: e.matmul(
                            pj[pb][:], lhsT=w[wb][:, kc * 128:(kc + 1) * 128], rhs=hT[:, kc, tsl],
                            start=(kc == 0), stop=(kc == KC - 1)), reads=[("w", wb), "hT"], writes=[("pj", pb)])
                    if hc >= 40:
                        S.act(lambda e, b=b, pb=pb: e.activation(out=obt[b][:], in_=pj[pb][:], func=AF.Copy),
                              reads=[("pj", pb)], writes=[("ob", b)])
                    else:
                        col = 0 if hc < 32 else 1
                        scl = (128.0 ** -0.5) if hc < 32 else 1.0
                        S.act(lambda e, b=b, pb=pb: e.activation(out=sq[b][:], in_=pj[pb][:], func=AF.Square),
                              reads=[("pj", pb)], writes=[("sq", b)])
                        S.dve(lambda e, b=b, pb=pb, col=col: e.tensor_scalar(
                            out=xw[b][:], in0=pj[pb][:], scalar1=qkn[:, col:col + 1], scalar2=None, op0=ALU.mult),
                            reads=[("pj", pb), "qkn"], writes=[("xw", b)])
                        S.pe(lambda e, b=b: e.matmul(ssq[:], lhsT=self.ones, rhs=sq[b][:], start=True, stop=True),
                             reads=[("sq", b), "cft"], writes=["ssq"])
                        S.pe(lambda e, b=b: e.matmul(px[:], lhsT=PiT, rhs=xw[b][:], start=True, stop=True),
                             reads=[("xw", b), "cft"], writes=["px"])
                        S.act(lambda e, b=b: e.activation(out=rs[b][:], in_=ssq[:], func=AF.Sqrt, scale=1.0 / 128,
                                                          bias=self.eps_ap), reads=["ssq", "cft"], writes=[("rs", b)])
                        S.dve(lambda e, b=b: e.reciprocal(out=rs[b][:], in_=rs[b][:]), reads=[("rs", b)],
                              writes=[("rs", b)])
                        S.dve(lambda e, b=b, tsl=tsl: e.tensor_tensor(out=t1[b][:], in0=xw[b][:], in1=cos[:, tsl],
                                                                      op=ALU.mult),
                              reads=[("xw", b), "cos"], writes=[("t1", b)])
                        S.dve(lambda e, b=b, tsl=tsl: e.tensor_tensor(out=t2[b][:], in0=px[:], in1=sin[:, tsl],
                                                                      op=ALU.mult),
                              reads=["px", "sin"], writes=[("t2", b)])
                        S.dve(lambda e, b=b: e.tensor_tensor(out=t1[b][:], in0=t1[b][:], in1=t2[b][:], op=ALU.add),
                               reads=[("t1", b), ("t2", b)], writes=[("t1", b)])
                        S.dve(lambda e, b=b, scl=scl: e.scalar_tensor_tensor(
                            out=obt[b][:], in0=t1[b][:], scalar=scl, in1=rs[b][:], op0=ALU.mult, op1=ALU.mult),
                            reads=[("t1", b), ("rs", b)], writes=[("ob", b)])
                    S.dma(qkvT[hc * 128:(hc + 1) * 128, tsl], obt[b][:], reads=[("ob", b)], writes=[], q="pool")
        if self.cfg.get('m1_stop') == 'A':
            return
        with self.phase() as (sb, ps):
            onesb = sb([128, 128], BF16)
            identb = sb([128, 128], BF16)
            S.dve(lambda e: e.tensor_copy(out=onesb[:], in_=self.ones), reads=["cft"], writes=["onesb"])
            S.dve(lambda e: e.tensor_copy(out=identb[:], in_=self.ident), reads=["cft"], writes=["identb"])
            kT = [sb([128, L], BF16) for _ in range(2)]
            vT = [sb([128, L], BF16) for _ in range(2)]
            V = [sb([128, 16, 128], BF16) for _ in range(2)]
            qT = [sb([128, L], BF16) for _ in range(2)]
            pT = [sb([128, TT], BF16) for _ in range(4)]
            rinv = [sb([128, TT], F32) for _ in range(2)]
            oT = [sb([128, TT], BF16) for _ in range(2)]
            s_ps = [ps([128, TT]) for _ in range(3)]
            o_ps = [ps([128, TT]) for _ in range(2)]
            r_ps = [ps([128, TT]) for _ in range(2)]
            si = 0
            pi = 0
            oi = 0
            qi = 0
            for kvh in range(8):
                kb = kvh % 2
                S.dma(kT[kb][:], qkvT[(32 + kvh) * 128:(33 + kvh) * 128, :], writes=[("kT", kb)])
                S.dma(V[kb][:], V_tm[:, kvh * 128:(kvh + 1) * 128].rearrange("(a p) d -> p a d", p=128),
                      writes=[("V", kb)])
                for hq in range(4):
                    h = kvh * 4 + hq
                    qb = qi % 2
                    qi += 1
                    S.dma(qT[qb][:], qkvT[h * 128:(h + 1) * 128, :], writes=[("qT", qb)])
                    for qt in range(NTT):
                        tsl = slice(qt * TT, (qt + 1) * TT)
                        ob = oi % 2
                        oi += 1
                        for sc in range(16):
                            sb_ = si % 3
                            si += 1
                            pb = pi % 4
                            pi += 1
                            S.pe(lambda e, kb=kb, qb=qb, sc=sc, sb_=sb_, tsl=tsl: e.matmul(
                                s_ps[sb_][:], lhsT=kT[kb][:, sc * 128:(sc + 1) * 128], rhs=qT[qb][:, tsl],
                                start=True, stop=True), reads=[("kT", kb), ("qT", qb)], writes=[("s_ps", sb_)])
                            S.act(lambda e, sb_=sb_, pb=pb: e.activation(out=pT[pb][:], in_=s_ps[sb_][:], func=AF.Exp),
                                  reads=[("s_ps", sb_)], writes=[("pT", pb)])
                            S.pe(lambda e, kb=kb, sc=sc, pb=pb, ob=ob: e.matmul(
                                o_ps[ob][:], lhsT=V[kb][:, sc, :], rhs=pT[pb][:], start=(sc == 0), stop=(sc == 15)),
                                reads=[("V", kb), ("pT", pb)], writes=[("o_ps", ob)])
                            S.pe(lambda e, pb=pb, ob=ob, sc=sc: e.matmul(
                                r_ps[ob][:], lhsT=onesb[:], rhs=pT[pb][:], start=(sc == 0), stop=(sc == 15)),
                                reads=["onesb", ("pT", pb)], writes=[("r_ps", ob)])
                        S.act(lambda e, ob=ob: e.activation(out=rinv[ob][:], in_=r_ps[ob][:], func=AF.Copy),
                              reads=[("r_ps", ob)], writes=[("rinv", ob)])
                        S.dve(lambda e, ob=ob: e.reciprocal(out=rinv[ob][:], in_=rinv[ob][:]),
                              reads=[("rinv", ob)], writes=[("rinv", ob)])
                        S.dve(lambda e, ob=ob: e.tensor_tensor(out=oT[ob][:], in0=o_ps[ob][:], in1=rinv[ob][:],
                                                               op=ALU.mult),
                              reads=[("o_ps", ob), ("rinv", ob)], writes=[("oT", ob)])
                        S.dma(attnT[h * 128:(h + 1) * 128, tsl], oT[ob][:], reads=[("oT", ob)], writes=[], q="pool")
        if self.cfg.get('m1_stop') == 'B':
            return
        self.proj_residual(attnT.rearrange("(k p) l -> p k l", p=128), wow, "wow", KC, xTv)

    def mix0(self, G, mw, xTv):
        S = self.S
        cft = self.cft
        ident, ones = self.ident, self.ones
        flat = lambda a: a.rearrange("r j -> (r j)")
        inw = flat(G["inw"]).rearrange("(n p k) -> n p k", p=128, k=D)
        dtw = flat(G["dtw"]).rearrange("(p k) -> p k", p=128)
        outw = flat(G["outw"]).rearrange("(n p k) -> n p k", p=128, k=8192)
        CLv = flat(G["dftc"]).rearrange("(p a l) -> p a l", p=128, l=L)
        SLv = flat(G["dfts"]).rearrange("(p a l) -> p a l", p=128, l=L)
        uT_scr = self.scratch("uT_scr", [2048, L], BF16)
        szT_scr = self.scratch("szT_scr", [6144, L], F32)
        x_tm = self.scratch("x_tm", [L, 6144], F32)
        B_tm = self.scratch("B_tm", [L, 1024], BF16)
        BT_scr = self.scratch("BT_scr", [1024, L], BF16)
        CT_scr = self.scratch("CT_scr", [1024, L], BF16)
        dt_scr = self.scratch("dt_scr", [L, 192], F32)
        mixin = self.scratch("mixin", [8192, L], BF16)
        FSCALE = float(1.0 / np.sqrt(2048.0 * 256.0))
        with self.phase() as (sb, ps):
            hT = sb([128, KC, L], BF16)
            nr = self.norm_setup(sb, ps, mw["mn0"])
            for tt in range(NTT):
                self.norm_tile(nr, xTv, tt, lambda kc, tt=tt: hT[:, kc, tt * TT:(tt + 1) * TT], "hT")
            cw = sb([128, 64, 5], F32)
            cb = sb([128, 64], F32)
            dtb = sb([128, 192], F32)
            S.dma(cw[:], mw["cw"][:, :, :], writes=["cw"])
            S.dma(cb[:], mw["cb"][:, :], writes=["cb"])
            S.dma(dtb[:], mw["dtb"][:, :], writes=["dtb"])
            w = [sb([128, D], BF16) for _ in range(2)]
            wdt = sb([128, 6144], BF16)
            pj = [ps([128, TT]) for _ in range(5)]
            tp = [ps([128, 512]) for _ in range(2)]
            cin = sb([128, L + 4], F32)
            xc = sb([128, L], F32)
            xcb = sb([128, L], BF16)
            stg = [sb([128, TT], F32) for _ in range(2)]
            stgb = [sb([128, TT], BF16) for _ in range(2)]
            tst = [sb([128, 4, 128], F32) for _ in range(2)]
            tstb = [sb([128, 4, 128], BF16) for _ in range(2)]
            S.dve(lambda e: e.memset(cin[:], 0.0), writes=["cin"])
            it = 0
            ti = 0
            for nci in range(128):
                wb = nci % 2
                S.dma(w[wb][:], inw[nci], reads=[("gath", "inw")], writes=[("w", wb)])
                for tt in range(NTT):
                    tsl = slice(tt * TT, (tt + 1) * TT)
                    pb = it % 5
                    b = it % 2
                    it += 1
                    for kc in range(KC):
                        S.pe(lambda e, wb=wb, kc=kc, pb=pb, tsl=tsl: e.matmul(
                            pj[pb][:], lhsT=w[wb][:, kc * 128:(kc + 1) * 128], rhs=hT[:, kc, tsl],
                            start=(kc == 0), stop=(kc == KC - 1)), reads=[("w", wb), "hT"], writes=[("pj", pb)])
                    if nci < 16:
                        S.act(lambda e, b=b, pb=pb: e.activation(out=stgb[b][:], in_=pj[pb][:], func=AF.Copy),
                              reads=[("pj", pb)], writes=[("stgb", b)])
                        S.dma(uT_scr[nci * 128:(nci + 1) * 128, tsl], stgb[b][:], reads=[("stgb", b)], q="pool")
                    elif nci < 64:
                        S.act(lambda e, b=b, pb=pb: e.activation(out=stg[b][:], in_=pj[pb][:], func=AF.Silu),
                              reads=[("pj", pb)], writes=[("stg", b)])
                        S.dma(szT_scr[(nci - 16) * 128:(nci - 15) * 128, tsl], stg[b][:], reads=[("stg", b)], q="pool")
                    else:
                        S.act(lambda e, pb=pb, tt=tt: e.activation(out=cin[:, 2 + tt * TT:2 + (tt + 1) * TT],
                                                                   in_=pj[pb][:], func=AF.Copy),
                              reads=[("pj", pb)], writes=["cin"])
                if nci >= 64:
                    ch = nci - 64
                    S.dve(lambda e, ch=ch: e.tensor_scalar(out=xc[:], in0=cin[:, 0:L], scalar1=cw[:, ch, 0:1],
                                                           scalar2=None, op0=ALU.mult),
                          reads=["cin", "cw"], writes=["xc"])
                    for j in range(1, 5):
                        S.dve(lambda e, ch=ch, j=j: e.scalar_tensor_tensor(
                            out=xc[:], in0=cin[:, j:j + L], scalar=cw[:, ch, j:j + 1], in1=xc[:],
                            op0=ALU.mult, op1=ALU.add), reads=["cin", "cw", "xc"], writes=["xc"])
                    S.act(lambda e, ch=ch: e.activation(out=xc[:], in_=xc[:], func=AF.Silu, bias=cb[:, ch:ch + 1]),
                          reads=["xc", "cb"], writes=["xc"])
                    if ch < 56:
                        for m4 in range(4):
                            tb = ti % 2
                            ti += 1
                            for j in range(4):
                                lt = m4 * 4 + j
                                S.pe(lambda e, tb=tb, j=j, lt=lt: e.transpose(
                                    out=tp[tb][:, j * 128:(j + 1) * 128], in_=xc[:, lt * 128:(lt + 1) * 128],
                                    identity=ident), reads=["xc", "cft"], writes=[("tp", tb)])
                            rows = slice(m4 * 512, (m4 + 1) * 512)
                            if ch < 48:
                                S.act(lambda e, tb=tb: e.activation(out=tst[tb][:].rearrange("p a c -> p (a c)"),
                                                                    in_=tp[tb][:], func=AF.Copy),
                                      reads=[("tp", tb)], writes=[("tst", tb)])
                                S.dma(x_tm[rows, ch * 128:(ch + 1) * 128].rearrange("(a p) c -> p a c", p=128),
                                      tst[tb][:], reads=[("tst", tb)], q="pool")
                            else:
                                gq = ch - 48
                                S.act(lambda e, tb=tb: e.activation(out=tstb[tb][:].rearrange("p a c -> p (a c)"),
                                                                    in_=tp[tb][:], func=AF.Copy),
                                      reads=[("tp", tb)], writes=[("tstb", tb)])
                                S.dma(B_tm[rows, gq * 128:(gq + 1) * 128].rearrange("(a p) c -> p a c", p=128),
                                      tstb[tb][:], reads=[("tstb", tb)], q="pool")
                    if ch >= 48:
                        S.dve(lambda e: e.tensor_copy(out=xcb[:], in_=xc[:]), reads=["xc"], writes=["xcb"])
                        dst = BT_scr if ch < 56 else CT_scr
                        gq = (ch - 48) % 8
                        S.dma(dst[gq * 128:(gq + 1) * 128, :], xcb[:], reads=["xcb"], q="pool")
            S.dma(wdt[:], dtw, reads=[("gath", "dtw")], writes=["wdt"])
            for m in range(16):
                pb = it % 5
                b = it % 2
                it += 1
                for kc in range(KC):
                    S.pe(lambda e, kc=kc, pb=pb, m=m: e.matmul(
                        pj[pb][:, 0:192], lhsT=hT[:, kc, m * 128:(m + 1) * 128], rhs=wdt[:, kc * 192:(kc + 1) * 192],
                        start=(kc == 0), stop=(kc == KC - 1)), reads=["wdt", "hT"], writes=[("pj", pb)])
                tq = stg[b][:, 0:192]
                aq = stg[b][:, 192:384]
                S.dve(lambda e, tq=tq, pb=pb: e.tensor_tensor(out=tq, in0=pj[pb][:, 0:192], in1=dtb[:], op=ALU.add),
                      reads=[("pj", pb), "dtb"], writes=[("stg", b)])
                S.act(lambda e, tq=tq, aq=aq: e.activation(out=aq, in_=tq, func=AF.Abs),
                      reads=[("stg", b)], writes=[("stg", b)])
                S.act(lambda e, aq=aq: e.activation(out=aq, in_=aq, func=AF.Exp, scale=-1.0),
                      reads=[("stg", b)], writes=[("stg", b)])
                S.act(lambda e, aq=aq: e.activation(out=aq, in_=aq, func=AF.Ln, bias=self.one_col),
                      reads=[("stg", b), "cft"], writes=[("stg", b)])
                S.dve(lambda e, tq=tq, aq=aq: e.scalar_tensor_tensor(out=tq, in0=tq, scalar=0.0, in1=aq,
                                                                     op0=ALU.max, op1=ALU.add),
                      reads=[("stg", b)], writes=[("stg", b)])
                S.dma(dt_scr[m * 128:(m + 1) * 128, :], tq, reads=[("stg", b)], q="pool")
        with self.phase() as (sb, ps):
            CL = sb([128, 16, L], BF16)
            SL = sb([128, 16, L], BF16)
            cdsd = sb([128, 2, 512], BF16)
            S.dma(CL[:], CLv, reads=[("gath", "dftc")], writes=["CL"])
            S.dma(SL[:], SLv, reads=[("gath", "dfts")], writes=["SL"])
            S.dma(cdsd[:], mw["cdsd"][:, :, :], writes=["cdsd"])
            uT = [sb([128, 2, L], BF16) for _ in range(2)]
            PQ = [sb([128, 16, 512], BF16) for _ in range(2)]
            stgbB = [sb([128, TT], BF16) for _ in range(2)]
            pq_ps = [ps([128, 512]) for _ in range(2)]
            y_ps = [ps([128, 512]) for _ in range(2)]
            yi = 0
            for g in range(8):
                gb = g % 2
                S.dma(uT[gb][:], uT_scr[g * 256:(g + 1) * 256, :].rearrange("(c p) l -> p c l", p=128),
                      writes=[("uT", gb)])
                for lt in range(16):
                    pb = lt % 2
                    for dch in range(2):
                        S.pe(lambda e, gb=gb, dch=dch, lt=lt, pb=pb: e.matmul(
                            pq_ps[pb][:], lhsT=uT[gb][:, dch, lt * 128:(lt + 1) * 128], rhs=cdsd[:, dch, :],
                            start=(dch == 0), stop=(dch == 1)), reads=[("uT", gb), "cdsd"], writes=[("pq_ps", pb)])
                    S.act(lambda e, gb=gb, lt=lt, pb=pb: e.activation(out=PQ[gb][:, lt, 0:256], in_=pq_ps[pb][:, 0:256],
                                                                      func=AF.Copy),
                          reads=[("pq_ps", pb)], writes=[("PQ", gb)])
                    S.dve(lambda e, gb=gb, lt=lt, pb=pb: e.tensor_scalar(
                        out=PQ[gb][:, lt, 256:512], in0=pq_ps[pb][:, 256:512], scalar1=-1.0, scalar2=None,
                        op0=ALU.mult), reads=[("pq_ps", pb)], writes=[("PQ", gb)])
                for dch in range(2):
                    for l4 in range(4):
                        yb = yi % 2
                        yi += 1
                        lsl = slice(l4 * 512, (l4 + 1) * 512)
                        for lt in range(16):
                            S.pe(lambda e, gb=gb, dch=dch, lt=lt, yb=yb, lsl=lsl: e.matmul(
                                y_ps[yb][:], lhsT=PQ[gb][:, lt, dch * 128:(dch + 1) * 128], rhs=CL[:, lt, lsl],
                                start=(lt == 0), stop=False), reads=[("PQ", gb), "CL"], writes=[("y_ps", yb)])
                            S.pe(lambda e, gb=gb, dch=dch, lt=lt, yb=yb, lsl=lsl: e.matmul(
                                y_ps[yb][:], lhsT=PQ[gb][:, lt, 256 + dch * 128:256 + (dch + 1) * 128],
                                rhs=SL[:, lt, lsl], start=False, stop=(lt == 15)),
                                reads=[("PQ", gb), "SL"], writes=[("y_ps", yb)])
                        S.dve(lambda e, yb=yb: e.tensor_scalar(out=stgbB[yb][:], in0=y_ps[yb][:], scalar1=FSCALE,
                                                               scalar2=None, op0=ALU.mult),
                              reads=[("y_ps", yb)], writes=[("stgbB", yb)])
                        r0 = g * 256 + dch * 128
                        S.dma(mixin[r0:r0 + 128, lsl], stgbB[yb][:], reads=[("stgbB", yb)], q="pool")
        with self.phase() as (sb, ps):
            Minc = cft[:, C_MINC:C_MINC + 128]
            Mdec = cft[:, C_MDEC:C_MDEC + 128]
            Sgt = cft[:, C_SGT:C_SGT + 128]
            Slt = cft[:, C_SLT:C_SLT + 128]
            Abc = sb([128, 192], F32)
            Dbc = sb([128, 96], F32)
            gnw = sb([128, 48], F32)
            tmpc = sb([128, 192], F32)
            S.dma(Abc[:], mw["alog"][:, :], writes=["Abc"])
            S.dma(Dbc[:], mw["dsk"][:, :], writes=["Dbc"])
            S.dma(gnw[:], mw["gnw"][:, :], writes=["gnw"])
            S.act(lambda e: e.activation(out=Abc[:], in_=Abc[:], func=AF.Exp), reads=["Abc"], writes=["Abc"])
            S.dve(lambda e: e.tensor_scalar(out=Abc[:], in0=Abc[:], scalar1=-1.0, scalar2=None, op0=ALU.mult),
                  reads=["Abc"], writes=["Abc"])
            dt_all = sb([128, 16, 192], F32)
            dtA = sb([128, 16, 192], F32)
            ecum = sb([128, 16, 192], F32)
            wdec = sb([128, 16, 192], F32)
            cdec = sb([128, 16, 192], F32)
            S.dma(dt_all[:], dt_scr.rearrange("(c p) h -> p c h", p=128), writes=["dt_all"])
            S.dve(lambda e: e.tensor_tensor(out=dtA[:], in0=dt_all[:],
                                            in1=Abc[:].unsqueeze(1).to_broadcast([128, 16, 192]), op=ALU.mult),
                  reads=["dt_all", "Abc"], writes=["dtA"])
            P3 = ps([128, 1536])
            PA = ps([128, 1024])
            PB = ps([128, 1024])
            PC = ps([128, 512])
            for c in range(16):
                S.pe(lambda e, c=c: e.matmul(PC[:, 0:96], lhsT=Minc, rhs=dtA[:, c, 0:96], start=True, stop=True),
                     reads=["dtA", "cft"], writes=["PC"])
                S.pe(lambda e, c=c: e.matmul(PC[:, 96:192], lhsT=Mdec, rhs=dtA[:, c, 96:192], start=True, stop=True),
                     reads=["dtA", "cft"], writes=["PC"])
                S.pe(lambda e, c=c: e.matmul(PC[:, 192:384], lhsT=ones, rhs=dtA[:, c, :], start=True, stop=True),
                     reads=["dtA", "cft"], writes=["PC"])
                S.act(lambda e, c=c: e.activation(out=ecum[:, c, :], in_=PC[:, 0:192], func=AF.Exp),
                      reads=["PC"], writes=["ecum"])
                S.act(lambda e, c=c: e.activation(out=cdec[:, c, :], in_=PC[:, 192:384], func=AF.Exp),
                      reads=["PC"], writes=["cdec"])
                S.act(lambda e: e.activation(out=tmpc[:], in_=PC[:, 0:192], func=AF.Copy),
                      reads=["PC"], writes=["tmpc"])
                S.dve(lambda e, c=c: e.tensor_tensor(out=wdec[:, c, :], in0=PC[:, 192:384], in1=tmpc[:],
                                                     op=ALU.subtract), reads=["PC", "tmpc"], writes=["wdec"])
                S.act(lambda e, c=c: e.activation(out=wdec[:, c, :], in_=wdec[:, c, :], func=AF.Exp),
                      reads=["wdec"], writes=["wdec"])
                S.dve(lambda e, c=c: e.tensor_tensor(out=wdec[:, c, :], in0=wdec[:, c, :], in1=dt_all[:, c, :],
                                                     op=ALU.mult), reads=["wdec", "dt_all"], writes=["wdec"])
            BT = [sb([128, L], BF16) for _ in range(2)]
            CT = [sb([128, L], BF16) for _ in range(2)]
            xg = [sb([128, 12, 64], F32) for _ in range(2)]
            Btm = [sb([128, 128], BF16) for _ in range(2)]
            CBm = [sb([128, 128], F32) for _ in range(2)]
            Dm = sb([128, 12, 128], F32)
            E = sb([128, 12, 128], F32)
            M = [sb([128, 12, 128], BF16) for _ in range(2)]
            xs = [sb([128, 12, 64], BF16) for _ in range(2)]
            xsd = [sb([128, 12, 64], BF16) for _ in range(2)]
            prev_f = sb([128, 12, 64], F32)
            prev_b = sb([128, 768], BF16)
            tmp = [sb([128, 12, 64], F32) for _ in range(2)]
            y_acc = sb([128, 16, 768], F32)
            szt = sb([128, 6, TT], F32)
            yg = sb([128, 6, TT], F32)
            sq = sb([128, TT], F32)
            rstd = sb([128, TT], F32)
            stgbC = [sb([128, TT], BF16) for _ in range(2)]
            k3 = lambda a: a.rearrange("p (k d) -> p k d", d=64)
            f2 = lambda a: a.rearrange("p k d -> p (k d)")
            it = 0
            si = 0
            for g in range(8):
                gb = g % 2
                S.dma(BT[gb][:], BT_scr[g * 128:(g + 1) * 128, :], writes=[("BT", gb)])
                S.dma(CT[gb][:], CT_scr[g * 128:(g + 1) * 128, :], writes=[("CT", gb)])
                for dr in range(2):
                    S.dve(lambda e: e.memset(prev_f[:], 0.0), writes=["prev_f"])
                    S.dve(lambda e: e.memset(prev_b[:], 0.0), writes=["prev_b"])
                    mk = Minc if dr == 0 else Mdec
                    sx = Sgt if dr == 0 else Slt
                    col0 = dr * 96 + g * 12
                    for ci in range(16):
                        c = ci if dr == 0 else 15 - ci
                        b = it % 2
                        it += 1
                        csl = slice(c * 128, (c + 1) * 128)
                        S.dma(f2(xg[b][:]), x_tm[csl, g * 768:(g + 1) * 768], writes=[("xg", b)])
                        S.dma(Btm[b][:], B_tm[csl, g * 128:(g + 1) * 128], writes=[("Btm", b)])
                        S.pe(lambda e, gb=gb, csl=csl: e.matmul(PC[:, 0:128], lhsT=BT[gb][:, csl], rhs=CT[gb][:, csl],
                                                               start=True, stop=True),
                             reads=[("BT", gb), ("CT", gb)], writes=["PC"])
                        S.dve(lambda e, b=b, mk=mk: e.tensor_tensor(out=CBm[b][:], in0=PC[:, 0:128], in1=mk,
                                                                    op=ALU.mult),
                              reads=["PC", "cft"], writes=[("CBm", b)])
                        S.dve(lambda e, mk=mk, c=c, col0=col0: e.tensor_tensor(
                            out=Dm[:], in0=mk.unsqueeze(1).to_broadcast([128, 12, 128]),
                            in1=dtA[:, c, col0:col0 + 12].unsqueeze(2).to_broadcast([128, 12, 128]), op=ALU.mult),
                            reads=["cft", "dtA"], writes=["Dm"])
                        Dmf = Dm[:].rearrange("p k t -> p (k t)")
                        for j in range(3):
                            S.pe(lambda e, sx=sx, j=j, Dmf=Dmf: e.matmul(
                                P3[:, j * 512:(j + 1) * 512], lhsT=sx, rhs=Dmf[:, j * 512:(j + 1) * 512],
                                start=True, stop=True), reads=["Dm", "cft"], writes=["P3"])
                        S.act(lambda e: e.activation(out=E[:].rearrange("p k t -> p (k t)"), in_=P3[:], func=AF.Exp),
                              reads=["P3"], writes=["E"])
                        S.dve(lambda e, b=b: e.tensor_tensor(
                            out=M[b][:], in0=E[:], in1=CBm[b][:].unsqueeze(1).to_broadcast([128, 12, 128]),
                            op=ALU.mult), reads=["E", ("CBm", b)], writes=[("M", b)])
                        S.dve(lambda e, b=b, c=c, col0=col0: e.tensor_tensor(
                            out=xs[b][:], in0=xg[b][:],
                            in1=dt_all[:, c, col0:col0 + 12].unsqueeze(2).to_broadcast([128, 12, 64]), op=ALU.mult),
                            reads=[("xg", b), "dt_all"], writes=[("xs", b)])
                        S.dve(lambda e, b=b, c=c, col0=col0: e.tensor_tensor(
                            out=xsd[b][:], in0=xg[b][:],
                            in1=wdec[:, c, col0:col0 + 12].unsqueeze(2).to_broadcast([128, 12, 64]), op=ALU.mult),
                            reads=[("xg", b), "wdec"], writes=[("xsd", b)])
                        for k in range(12):
                            S.pe(lambda e, b=b, k=k: e.matmul(PA[:, k * 64:(k + 1) * 64], lhsT=M[b][:, k, :],
                                                             rhs=xs[b][:, k, :], start=True, stop=True),
                                 reads=[("M", b), ("xs", b)], writes=["PA"])
                        S.pe(lambda e, gb=gb, csl=csl: e.matmul(PB[:, 0:512], lhsT=CT[gb][:, csl], rhs=prev_b[:, 0:512],
                                                               start=True, stop=True),
                             reads=[("CT", gb), "prev_b"], writes=["PB"])
                        S.pe(lambda e, gb=gb, csl=csl: e.matmul(PB[:, 512:768], lhsT=CT[gb][:, csl],
                                                               rhs=prev_b[:, 512:768], start=True, stop=True),
                             reads=[("CT", gb), "prev_b"], writes=["PB"])
                        if dr == 0:
                            S.dve(lambda e, b=b, c=c, g=g: e.tensor_tensor(
                                out=k3(y_acc[:, c, :]), in0=xg[b][:],
                                in1=Dbc[:, g * 12:(g + 1) * 12].unsqueeze(2).to_broadcast([128, 12, 64]),
                                op=ALU.mult), reads=[("xg", b), "Dbc"], writes=[("yacc", c)])
                        S.dve(lambda e, b=b, c=c, col0=col0: e.tensor_tensor(
                            out=tmp[b][:], in0=k3(PB[:, 0:768]),
                            in1=ecum[:, c, col0:col0 + 12].unsqueeze(2).to_broadcast([128, 12, 64]), op=ALU.mult),
                            reads=["PB", "ecum"], writes=[("tmp", b)])
                        S.dve(lambda e, b=b: e.tensor_tensor(out=tmp[b][:], in0=tmp[b][:], in1=k3(PA[:, 0:768]),
                                                             op=ALU.add),
                              reads=[("tmp", b), "PA"], writes=[("tmp", b)])
                        S.dve(lambda e, b=b, c=c: e.tensor_tensor(out=y_acc[:, c, :], in0=y_acc[:, c, :],
                                                                   in1=f2(tmp[b][:]), op=ALU.add),
                               reads=[("tmp", b), ("yacc", c)], writes=[("yacc", c)])
                        S.pe(lambda e, b=b: e.matmul(PB[:, 0:512], lhsT=Btm[b][:], rhs=f2(xsd[b][:])[:, 0:512],
                                                     start=True, stop=True),
                             reads=[("Btm", b), ("xsd", b)], writes=["PB"])
                        S.pe(lambda e, b=b: e.matmul(PB[:, 512:768], lhsT=Btm[b][:], rhs=f2(xsd[b][:])[:, 512:768],
                                                     start=True, stop=True),
                             reads=[("Btm", b), ("xsd", b)], writes=["PB"])
                        S.dve(lambda e, c=c, col0=col0: e.tensor_tensor(
                            out=prev_f[:], in0=prev_f[:],
                            in1=cdec[:, c, col0:col0 + 12].unsqueeze(2).to_broadcast([128, 12, 64]), op=ALU.mult),
                            reads=["prev_f", "cdec"], writes=["prev_f"])
                        S.dve(lambda e: e.tensor_tensor(out=prev_f[:], in0=prev_f[:], in1=k3(PB[:, 0:768]),
                                                        op=ALU.add), reads=["prev_f", "PB"], writes=["prev_f"])
                        S.act(lambda e: e.activation(out=prev_b[:], in_=f2(prev_f[:]), func=AF.Copy),
                              reads=["prev_f"], writes=["prev_b"])
                for tt in range(NTT):
                    tsl = slice(tt * TT, (tt + 1) * TT)
                    S.dma(szt[:], szT_scr[g * 768:(g + 1) * 768, tsl].rearrange("(j p) l -> p j l", p=128),
                          writes=["szt"])
                    for j in range(6):
                        for cc in range(4):
                            c = tt * 4 + cc
                            S.pe(lambda e, c=c, cc=cc, j=j: e.transpose(
                                out=P3[:, cc * 128:(cc + 1) * 128], in_=y_acc[:, c, j * 128:(j + 1) * 128],
                                identity=ident), reads=[("yacc", c), "cft"], writes=["P3"])
                        S.dve(lambda e, j=j: e.tensor_tensor(out=yg[:, j, :], in0=P3[:, 0:512], in1=szt[:, j, :],
                                                             op=ALU.mult), reads=["P3", "szt"], writes=["yg"])
                        S.act(lambda e, j=j: e.activation(out=sq[:], in_=yg[:, j, :], func=AF.Square),
                              reads=["yg"], writes=["sq"])
                        S.pe(lambda e, j=j: e.matmul(PC[:], lhsT=ones, rhs=sq[:], start=(j == 0), stop=(j == 5)),
                             reads=["sq", "cft"], writes=["PC"])
                    S.act(lambda e: e.activation(out=rstd[:], in_=PC[:], func=AF.Sqrt, scale=1.0 / 768,
                                                 bias=self.eps_ap), reads=["PC", "cft"], writes=["rstd"])
                    S.dve(lambda e: e.reciprocal(out=rstd[:], in_=rstd[:]), reads=["rstd"], writes=["rstd"])
                    for j in range(6):
                        sbb = si % 2
                        si += 1
                        S.dve(lambda e, j=j, sbb=sbb, g=g: e.scalar_tensor_tensor(
                            out=stgbC[sbb][:], in0=yg[:, j, :], scalar=gnw[:, g * 6 + j:g * 6 + j + 1], in1=rstd[:],
                            op0=ALU.mult, op1=ALU.mult), reads=["yg", "gnw", "rstd"], writes=[("stgbC", sbb)])
                        r0 = 2048 + g * 768 + j * 128
                        S.dma(mixin[r0:r0 + 128, tsl], stgbC[sbb][:], reads=[("stgbC", sbb)], q="pool")
        self.proj_residual(mixin.rearrange("(k p) l -> p k l", p=128), outw, "outw", 64, xTv)


def _tile_stationary(W, kchunks, nchunks):
    K, N = W.shape
    t = W.reshape(kchunks, 128, nchunks, 128).transpose(2, 1, 0, 3)
    return np.ascontiguousarray(t).reshape(nsh(), 128, -1)


DEFAULT_CFG = dict(stages=["ffn00", "mix0", "ffn01", "ffn10", "mix1", "ffn11"], final_norm=True)


def _consts():
    cf = np.zeros((128, NCF), np.float32)
    cf[:, 0:128] = np.eye(128, dtype=np.float32)
    cf[:, 128:256] = 1.0
    cf[:, 256] = EPS
    r = np.arange(128)[:, None]
    c = np.arange(128)[None, :]
    cf[:, C_MINC:C_MINC + 128] = (r <= c)
    cf[:, C_MDEC:C_MDEC + 128] = (r >= c)
    cf[:, C_SGT:C_SGT + 128] = (r > c)
    cf[:, C_SLT:C_SLT + 128] = (r < c)
    pit = np.zeros((128, 128), np.float32)
    for dp in range(128):
        if dp % 64 < 32:
            pit[dp + 32, dp] = -1.0
        else:
            pit[dp - 32, dp] = 1.0
    cf[:, C_PIT:C_PIT + 128] = pit
    return cf


def _rope_tables():
    l = np.arange(L)
    pos = np.stack([l // 64, l % 64], 0).astype(np.float32)
    inv = (10000.0 ** (-(np.arange(0, 64, 2, dtype=np.float32) / 64.0))).astype(np.float32)
    ang = pos[:, None, :] * inv[None, :, None]
    ang = np.concatenate([ang, ang], axis=1).reshape(128, L)
    return np.cos(ang).astype(np.float32), np.sin(ang).astype(np.float32)


def _dft_tables():
    l = np.arange(L, dtype=np.int64)
    m = (l[:, None] * l[None, :]) % L
    a = 2.0 * np.pi * m.astype(np.float64) / L
    cl = np.cos(a).astype(np.float32)
    sl = np.sin(a).astype(np.float32)
    lay = lambda t: np.ascontiguousarray(t.reshape(16, 128, L).transpose(1, 0, 2)).reshape(nsh(), 128, -1)
    d = np.arange(256, dtype=np.int64)
    md = (d[:, None] * d[None, :]) % 256
    ad = 2.0 * np.pi * md.astype(np.float64) / 256
    cdsd = np.concatenate([np.cos(ad), np.sin(ad)], axis=1).astype(np.float32)
    cdsd = np.ascontiguousarray(cdsd.reshape(2, 128, 512).transpose(1, 0, 2)).astype(ml_dtypes.bfloat16)
    return lay(cl), lay(sl), cdsd


def _pc(v, n):
    return np.ascontiguousarray(np.asarray(v, np.float32).reshape(n, 128).T)


def _rep(v):
    v = np.asarray(v, np.float32).reshape(1, -1)
    return np.ascontiguousarray(np.broadcast_to(v, (128, v.shape[1])))


def kernel(cfg=None, **inp):
    cfg = dict(DEFAULT_CFG if cfg is None else cfg)
    stages = cfg["stages"]
    prog = Prog(cfg)
    nc = prog.build()
    x = np.asarray(inp["x"], dtype=np.float32)
    shared = {"cf32": _consts()}
    per_core = [dict() for _ in range(NCORE)]

    def put(name, arr8):
        for c in range(NCORE):
            per_core[c][name] = arr8[c % nsh()]

    for t in stages:
        if t.startswith("ffn"):
            i, s = int(t[3]), int(t[4])
            put(f"ffn{i}{s}g", _tile_stationary(np.asarray(inp["ffn_w_gate"][i, s]), KC, FC))
            put(f"ffn{i}{s}u", _tile_stationary(np.asarray(inp["ffn_w_up"][i, s]), KC, FC))
            put(f"ffn{i}{s}d", _tile_stationary(np.asarray(inp["ffn_w_down"][i, s]), FC, KC))
            shared[f"ffn{i}{s}n"] = _pc(inp["ffn_norm"][i, s], KC)
        elif t == "mix0":
            W = np.asarray(inp["hyb_in_proj"][0])
            put("inw", _tile_stationary(W[:, :16384], KC, 128))
            dtw = np.ascontiguousarray(W[:, 16384:].reshape(KC, 128, 192).transpose(1, 0, 2))
            put("dtw", dtw.reshape(nsh(), 128, -1))
            put("outw", _tile_stationary(np.asarray(inp["hyb_out_proj"][0]), 64, KC))
            cl, sl, cdsd = _dft_tables()
            put("dftc", cl)
            put("dfts", sl)
            shared["cdsd"] = cdsd
            shared["mn0"] = _pc(inp["mix_norm"][0], KC)
            cw = np.asarray(inp["ssd_conv_w"][0], np.float32)
            shared["cw"] = np.ascontiguousarray(cw.reshape(5, 64, 128).transpose(2, 1, 0))
            shared["cb"] = _pc(inp["ssd_conv_b"][0], 64)
            shared["dtb"] = _rep(inp["ssd_dt_bias"][0])
            shared["alog"] = _rep(inp["ssd_A_log"][0])
            shared["dsk"] = _rep(inp["ssd_D"][0])
            shared["gnw"] = _pc(inp["ssd_gnorm"][0], 48)
        elif t == "mix1":
            put("qkvw", _tile_stationary(np.asarray(inp["attn_w_qkv"][0]), KC, 48))
            put("wow", _tile_stationary(np.asarray(inp["attn_w_o"][0]), KC, KC))
            shared["mn1"] = _pc(inp["mix_norm"][1], KC)
            shared["qkn"] = np.ascontiguousarray(np.stack([np.asarray(inp["attn_q_norm"][0], np.float32),
                                                           np.asarray(inp["attn_k_norm"][0], np.float32)], axis=1))
            shared["cos"], shared["sin"] = _rope_tables()
    if cfg.get("final_norm", True):
        shared["finw"] = _rep(inp["final_norm"])
    in_maps = []
    for c in range(NCORE):
        m = dict(shared)
        m.update(per_core[c])
        m["x"] = np.ascontiguousarray(x[c])
        in_maps.append(m)
    if cfg.get('return_prog'):
        return nc, in_maps
    res = run_bass_kernel_spmd(nc, in_maps, core_ids=list(range(NCORE)))
    return np.stack([np.asarray(r["out"]) for r in res.results], axis=0)
```

```python
import contextlib
import numpy as np
import ml_dtypes
import concourse.bass as bass
import concourse.mybir as mybir
from concourse.bass_utils import run_bass_kernel_spmd

F32 = mybir.dt.float32
BF16 = mybir.dt.bfloat16
AF = mybir.ActivationFunctionType
ALU = mybir.AluOpType
AX = mybir.AxisListType

NCORE = 8
REPLICATE = True


def nsh():
    return 1 if REPLICATE else NCORE
D = 4096
L = 2048
DFF = 11008
KC = D // 128
FC = DFF // 128
TT = 512
NTT = L // TT
EPS = 1e-6
FFN_SH = 44032
NCF = 128 * 7 + 8
C_MINC, C_MDEC, C_SGT, C_SLT, C_PIT = 264, 392, 520, 648, 776
M0_W = [("inw", 65536), ("dtw", 768), ("outw", 32768), ("dftc", 4096), ("dfts", 4096)]
M1_W = [("qkvw", 24576), ("wow", 16384)]
M0_C = [("mn0", [128, 32], F32), ("cw", [128, 64, 5], F32), ("cb", [128, 64], F32), ("dtb", [128, 192], F32),
        ("alog", [128, 192], F32), ("dsk", [128, 96], F32), ("gnw", [128, 48], F32), ("cdsd", [128, 2, 512], BF16)]
M1_C = [("mn1", [128, 32], F32), ("qkn", [128, 2], F32), ("cos", [128, 2048], F32), ("sin", [128, 2048], F32)]

SAME_ENG_SYNC = True


class Sched:
    ENGS = ("pe", "act", "dve", "pool", "sp")
    KDMA = {"sp": 16, "pool": 8, "act": 8, "pe": 8, "dve": 8}
    PSUM_KEYS = {"pj", "pq_ps", "PC", "P3", "PA", "PB", "tp", "tpo", "ssq", "px", "o_ps", "r_ps", "s_ps",
                 "g_ps", "u_ps", "y_ps", "ssq_ps"}
    EPOCH = 4500

    def __init__(self, nc):
        self.nc = nc
        self.ops = []
        self.last_writer = {}
        self.readers_c = {}
        self.readers_d = {}
        self.n_cc = 0

    def add(self, eng, fn, reads=(), writes=(), kind="c", nophase=False):
        i = len(self.ops)
        deps = set()
        reads = tuple(reads) + (() if nophase else ("__phase__",))
        for k in reads:
            w = self.last_writer.get(k)
            if w is not None:
                deps.add(w)
            if (k if isinstance(k, str) else k[0]) in self.PSUM_KEYS:
                for re_, ri in self.readers_c.get(k, {}).items():
                    if re_ != eng:
                        deps.add(ri)
        for k in writes:
            w = self.last_writer.get(k)
            if w is not None:
                deps.add(w)
            deps.update(self.readers_c.get(k, {}).values())
            deps.update(self.readers_d.get(k, ()))
        for k in reads:
            if kind == "c":
                self.readers_c.setdefault(k, {})[eng] = i
            else:
                if k != "__phase__" or True:
                    self.readers_d.setdefault(k, []).append(i)
        for k in writes:
            self.last_writer[k] = i
            self.readers_c[k] = {}
            self.readers_d[k] = []
        cc = None
        if kind == "cc":
            cc = self.n_cc
            self.n_cc += 1
        self.ops.append(dict(eng=eng, fn=fn, deps=deps, kind=kind, cc=cc))
        return i

    def pe(self, fn, reads=(), writes=()):
        return self.add("pe", fn, reads, writes)

    def act(self, fn, reads=(), writes=()):
        return self.add("act", fn, reads, writes)

    def dve(self, fn, reads=(), writes=()):
        return self.add("dve", fn, reads, writes)

    def pool(self, fn, reads=(), writes=()):
        return self.add("pool", fn, reads, writes)

    def dma(self, out, in_, reads=(), writes=(), q="sp", nophase=False):
        return self.add(q, lambda e: e.dma_start(out=out, in_=in_), reads, writes, kind="d", nophase=nophase)

    def barrier(self, scratch):
        self.add("dve", lambda e: e.memset(scratch, 0.0), reads=(), writes=("__phase__",))

    def emit(self):
        nc = self.nc
        ops = self.ops
        need = [False] * len(ops)
        for o in ops:
            for d in o["deps"]:
                if ops[d]["eng"] == "pe" and o["eng"] == "pe" and ops[d]["kind"] == "c" and o["kind"] == "c":
                    continue
                if (not SAME_ENG_SYNC) and ops[d]["eng"] == o["eng"] and ops[d]["kind"] == "c" and o["kind"] == "c":
                    continue
                need[d] = True
        cnt = {e: 0 for e in self.ENGS}
        dcnt = {e: 0 for e in self.ENGS}
        dhist = {e: [] for e in self.ENGS}
        by_eng = {e: [] for e in self.ENGS}
        for i, o in enumerate(ops):
            e = o["eng"]
            by_eng[e].append(i)
            o["ticket"] = None
            o["prev"] = None
            if o["kind"] == "c":
                if need[i]:
                    cnt[e] += 1
                    o["ticket"] = (("c", e, (cnt[e] - 1) // self.EPOCH), (cnt[e] - 1) % self.EPOCH + 1, 1)
            elif o["kind"] == "d":
                j = dcnt[e]
                dcnt[e] += 1
                kd = self.KDMA[e]
                o["ticket"] = (("d", e, j % kd), 16 * (j // kd + 1), 16)
                if j >= kd:
                    o["prev"] = dhist[e][j - kd]
                dhist[e].append(i)
            else:
                o["ticket"] = (("cc", o["cc"]), 1, 1)
        with contextlib.ExitStack() as st:
            sems = {}

            def sem(key):
                if key not in sems:
                    sems[key] = st.enter_context(nc.semaphore("s_" + "_".join(str(x) for x in key)))
                return sems[key]

            for e in self.ENGS:
                for ep in range((cnt[e] + self.EPOCH - 1) // self.EPOCH + 1):
                    sem(("c", e, ep))
                if dcnt[e]:
                    for s in range(self.KDMA[e]):
                        sem(("d", e, s))
            for c in range(self.n_cc):
                sem(("cc", c))
            block = st.enter_context(nc.Block())

            def run(engname, eng):
                waited = {}
                wep = {}
                for i in by_eng[engname]:
                    o = ops[i]
                    waits = {}
                    dl = list(o["deps"])
                    if o["prev"] is not None:
                        dl.append(o["prev"])
                    for d in dl:
                        od = ops[d]
                        if od["kind"] == "c" and o["kind"] == "c" and od["eng"] == engname:
                            if engname == "pe" or not SAME_ENG_SYNC:
                                continue
                        key, val, _ = od["ticket"]
                        if key[0] == "c" and wep.get(key[1], -1) > key[2]:
                            continue
                        if waited.get(key, 0) >= val:
                            continue
                        if waits.get(key, 0) < val:
                            waits[key] = val
                    for key, val in waits.items():
                        eng.wait_ge(sem(key), val)
                        waited[key] = val
                        if key[0] == "c":
                            wep[key[1]] = max(wep.get(key[1], -1), key[2])
                    ins = o["fn"](eng)
                    if o["ticket"] is not None:
                        key, val, inc = o["ticket"]
                        ins.then_inc(sem(key), inc)
                for i in by_eng[engname]:
                    o = ops[i]
                    if o["kind"] in ("d", "cc"):
                        key, val, _ = o["ticket"]
                        if waited.get(key, 0) < val:
                            waited[key] = val
                final = {}
                for i in by_eng[engname]:
                    o = ops[i]
                    if o["kind"] in ("d", "cc"):
                        key, val, _ = o["ticket"]
                        final[key] = max(final.get(key, 0), val)
                for key, val in final.items():
                    eng.wait_ge(sem(key), val)

            @block.tensor
            def _(eng):
                run("pe", eng)

            @block.scalar
            def _(eng):
                run("act", eng)

            @block.vector
            def _(eng):
                run("dve", eng)

            @block.gpsimd
            def _(eng):
                run("pool", eng)

            @block.sync
            def _(eng):
                run("sp", eng)


class Prog:
    def __init__(self, cfg):
        self.cfg = cfg
        self.nc = bass.Bass("TRN2", target_bir_lowering=False)
        self.S = Sched(self.nc)
        self.ins = {}
        self._uid = 0

    def uid(self, p):
        self._uid += 1
        return f"{p}{self._uid}"

    def ext_in(self, name, shape, dt=F32):
        self.ins[name] = (shape, dt)
        return self.nc.dram_tensor(name, list(shape), dt, kind="ExternalInput").ap()

    def scratch(self, name, shape, dt):
        return self.nc.dram_tensor(name, list(shape), dt).ap()

    @contextlib.contextmanager
    def phase(self):
        with contextlib.ExitStack() as st:
            def sb(shape, dt, name=None):
                return st.enter_context(self.nc.sbuf_tensor(name or self.uid("t"), list(shape), dt))

            def ps(shape, dt=F32, name=None):
                return st.enter_context(self.nc.psum_tensor(name or self.uid("p"), list(shape), dt))
            yield sb, ps
            self.S.barrier(self.bar_scratch[:])

    def cast_weight(self, name, src_ap, ncols, ntiles, grp):
        S = self.S
        gath = self.scratch(name + "_g", [128, ncols], BF16)
        kw = ncols // ntiles
        sv = src_ap.rearrange("r j -> (r j)").rearrange("(f p k) -> f p k", p=128, k=kw)
        dv = gath.rearrange("r j -> (r j)").rearrange("(f p k) -> f p k", p=128, k=kw)
        ng = (ntiles + grp - 1) // grp
        for g in range(ng):
            f0, f1 = g * grp, min(ntiles, (g + 1) * grp)
            S.dma(dv[f0:f1], sv[f0:f1], writes=[("gath", name, g)], q="pool", nophase=True)
        self.gw[name] = grp
        return gath

    def gk(self, name, tile):
        return ("gath", name, tile // self.gw[name])

    def gather_weight(self, name, shard_ap, ncols, stage, stage_bf, piece):
        S = self.S
        NS = nsh()
        gath = self.scratch(name + "_g", [128 * NS, ncols], BF16)
        bounce = gath if NS == 1 else self.scratch(name + "_b", [128, ncols], BF16)
        wkey = ("gath", name) if NS == 1 else ("bounce", name)
        npieces = (ncols + piece - 1) // piece
        for j in range(npieces):
            c0 = j * piece
            w = min(piece, ncols - c0)
            sl = self._gw_i % len(stage)
            self._gw_i += 1
            sf, sbf = stage[sl], stage_bf[sl]
            S.dma(sf[:, :w], shard_ap[:, c0:c0 + w], writes=[("gwf", sl)], q="sp")
            S.dve(lambda e, sf=sf, sbf=sbf, w=w: e.tensor_copy(out=sbf[:, :w], in_=sf[:, :w]),
                  reads=[("gwf", sl)], writes=[("gwb", sl)])
            S.dma(bounce[:, c0:c0 + w], sbf[:, :w], reads=[("gwb", sl)], writes=[(wkey[0], name, j)], q="act")
        if NS == 1:
            S.dve(lambda e: e.memset(self.bar_scratch[:, 0:4], 0.0),
                  reads=[("gath", name, j) for j in range(npieces)], writes=[("gath", name)])
        else:
            S.dve(lambda e: e.memset(self.bar_scratch[:, 0:4], 0.0),
                  reads=[("bounce", name, j) for j in range(npieces)], writes=[("bounce", name)])
            S.add("pool", lambda e: e.collective_compute("AllGather", ALU.bypass,
                                                         replica_groups=[list(range(NCORE))],
                                                         ins=[bounce.opt()], outs=[gath.opt()]),
                  reads=[("bounce", name)], writes=[("gath", name)], kind="cc")
        return gath

    def build(self):
        cfg = self.cfg
        nc, S = self.nc, self.S
        stages = cfg["stages"]
        ffn_list = [(int(t[3]), int(t[4])) for t in stages if t.startswith("ffn")]
        x_in = self.ext_in("x", [L, D])
        cf = self.ext_in("cf32", [128, NCF])
        ffn_w = {}
        ffn_nw = {}
        for (i, s) in ffn_list:
            for nm in ("g", "u", "d"):
                ffn_w[(i, s, nm)] = self.ext_in(f"ffn{i}{s}{nm}", [128, FFN_SH * 8 // nsh()])
            ffn_nw[(i, s)] = self.ext_in(f"ffn{i}{s}n", [128, KC])
        mw = {}
        if "mix0" in stages:
            for nm, ncols in M0_W:
                mw[nm] = self.ext_in(nm, [128, ncols * 8 // nsh()])
            for nm, shp, dt in M0_C:
                mw[nm] = self.ext_in(nm, shp, dt)
        if "mix1" in stages:
            for nm, ncols in M1_W:
                mw[nm] = self.ext_in(nm, [128, ncols * 8 // nsh()])
            for nm, shp, dt in M1_C:
                mw[nm] = self.ext_in(nm, shp, dt)
        fin_w = self.ext_in("finw", [128, D]) if cfg.get("final_norm", True) else None
        out = nc.dram_tensor("out", [L, D], F32, kind="ExternalOutput").ap()
        xT = self.scratch("xT", [D, L], F32)
        xTv = xT.rearrange("(k p) l -> p k l", p=128)

        with contextlib.ExitStack() as gst:
            self.bar_scratch = gst.enter_context(nc.sbuf_tensor("bar_scr", [128, 8], F32))
            cft = gst.enter_context(nc.sbuf_tensor("cft", [128, NCF], F32))
            S.dma(cft[:], cf[:, :], writes=["cft"])
            self.cft = cft
            ident = cft[:, 0:128]
            ones = cft[:, 128:256]
            self.ident, self.ones = ident, ones
            self.eps_ap = cft[:, 256:257]
            self.one_col = cft[:, 128:129]

            gathered = {}
            self.gw = {}
            assert REPLICATE
            for t in stages:
                if t.startswith("ffn"):
                    i, s = int(t[3]), int(t[4])
                    ng = (FC + 7) // 8
                    names = {nm: f"w{i}{s}{nm}" for nm in ("g", "u", "d")}
                    for nm in ("g", "u"):
                        gathered[(i, s, nm)] = self.scratch(names[nm] + "_g", [128, FFN_SH * 8], BF16)
                        self.gw[names[nm]] = 8
                    for g in range(ng):
                        for nm in ("g", "u"):
                            f0, f1 = g * 8, min(FC, (g + 1) * 8)
                            sv = ffn_w[(i, s, nm)].rearrange("r j -> (r j)").rearrange("(f p k) -> f p k", p=128, k=D)
                            dv = gathered[(i, s, nm)].rearrange("r j -> (r j)").rearrange("(f p k) -> f p k", p=128, k=D)
                            S.dma(dv[f0:f1], sv[f0:f1], writes=[("gath", names[nm], g)], q="pool", nophase=True)
                    gathered[(i, s, "d")] = self.cast_weight(names["d"], ffn_w[(i, s, "d")], FFN_SH * 8, KC, 2)
                elif t == "mix0":
                    gathered["inw"] = self.cast_weight("inw", mw["inw"], 65536 * 8, 128, 8)
                    gathered["dtw"] = self.cast_weight("dtw", mw["dtw"], 768 * 8, 1, 1)
                    gathered["dftc"] = self.cast_weight("dftc", mw["dftc"], 4096 * 8, 1, 1)
                    gathered["dfts"] = self.cast_weight("dfts", mw["dfts"], 4096 * 8, 1, 1)
                    gathered["outw"] = self.cast_weight("outw", mw["outw"], 32768 * 8, 32, 4)
                elif t == "mix1":
                    gathered["qkvw"] = self.cast_weight("qkvw", mw["qkvw"], 24576 * 8, 48, 8)
                    gathered["wow"] = self.cast_weight("wow", mw["wow"], 16384 * 8, 32, 8)
            with self.phase() as (sb, ps):
                xrow = [sb([128, D], F32) for _ in range(2)]
                xst = [sb([128, KC, 128], F32) for _ in range(2)]
                tp = [ps([128, 512]) for _ in range(2)]
                for m in range(L // 128):
                    b = m % 2
                    S.dma(xrow[b][:], x_in[m * 128:(m + 1) * 128, :], writes=[("xrow", b)])
                    for k4 in range(KC // 4):
                        pb = k4 % 2
                        for j in range(4):
                            kc = k4 * 4 + j
                            S.pe(lambda e, b=b, pb=pb, j=j, kc=kc: e.transpose(
                                out=tp[pb][:, j * 128:(j + 1) * 128], in_=xrow[b][:, kc * 128:(kc + 1) * 128],
                                identity=ident), reads=[("xrow", b), "cft"], writes=[("tp", pb)])
                        S.act(lambda e, b=b, pb=pb, k4=k4: e.activation(
                            out=xst[b][:, k4 * 4:(k4 + 1) * 4, :].rearrange("p k t -> p (k t)"), in_=tp[pb][:],
                            func=AF.Copy), reads=[("tp", pb)], writes=[("xst", b)])
                    S.dma(xTv[:, :, m * 128:(m + 1) * 128], xst[b][:], reads=[("xst", b)],
                          writes=[("xT", m // 4)])

            for t in stages:
                if t.startswith("ffn"):
                    i, s = int(t[3]), int(t[4])
                    self.ffn_phase(gathered[(i, s, "g")], gathered[(i, s, "u")], gathered[(i, s, "d")],
                                   ffn_nw[(i, s)], (f"w{i}{s}g", f"w{i}{s}u", f"w{i}{s}d"), xTv, ones)
                elif t == "mix0":
                    self.mix0(gathered, mw, xTv)
                elif t == "mix1":
                    self.mix1(gathered, mw, xTv)

            with self.phase() as (sb, ps):
                xko = [sb([128, KC, 128], F32) for _ in range(2)]
                orow = [sb([128, D], F32) for _ in range(2)]
                tpo = [ps([128, 512]) for _ in range(2)]
                junk = sb([128, D], F32)
                ssq = sb([128, 2], F32)
                rstd = sb([128, 2], F32)
                if fin_w is not None:
                    fw = sb([128, D], F32)
                    S.dma(fw[:], fin_w[:, :], writes=["fw"])
                for m in range(L // 128):
                    b = m % 2
                    S.dma(xko[b][:], xTv[:, :, m * 128:(m + 1) * 128], reads=[("xT", m // 4)], writes=[("xk", b)])
                    for k4 in range(KC // 4):
                        pb = k4 % 2
                        for j in range(4):
                            kc = k4 * 4 + j
                            S.pe(lambda e, b=b, pb=pb, j=j, kc=kc: e.transpose(
                                out=tpo[pb][:, j * 128:(j + 1) * 128], in_=xko[b][:, kc, :], identity=ident),
                                reads=[("xk", b), "cft"], writes=[("tpo", pb)])
                        S.act(lambda e, b=b, pb=pb, k4=k4: e.activation(
                            out=orow[b][:, k4 * 512:(k4 + 1) * 512], in_=tpo[pb][:], func=AF.Copy),
                            reads=[("tpo", pb)], writes=[("orow", b)])
                    if fin_w is not None:
                        S.act(lambda e, b=b: e.activation(out=junk[:], in_=orow[b][:], func=AF.Square,
                                                          accum_out=ssq[:, b:b + 1]),
                              reads=[("orow", b)], writes=["junk", ("ssq", b)])
                        S.act(lambda e, b=b: e.activation(out=rstd[:, b:b + 1], in_=ssq[:, b:b + 1], func=AF.Sqrt,
                                                          scale=1.0 / D, bias=self.eps_ap),
                              reads=[("ssq", b)], writes=[("rstd", b)])
                        S.dve(lambda e, b=b: e.reciprocal(out=rstd[:, b:b + 1], in_=rstd[:, b:b + 1]),
                              reads=[("rstd", b)], writes=[("rstd", b)])
                        S.dve(lambda e, b=b: e.scalar_tensor_tensor(out=orow[b][:], in0=orow[b][:],
                                                                    scalar=rstd[:, b:b + 1], in1=fw[:],
                                                                    op0=ALU.mult, op1=ALU.mult),
                              reads=[("orow", b), ("rstd", b), "fw"], writes=[("orow", b)])
                    S.dma(out[m * 128:(m + 1) * 128, :], orow[b][:], reads=[("orow", b)], writes=[("out", m)])
            S.emit()
        return nc

    def ffn_phase(self, gW, uW, dW, nw_ap, wnames, xTv, ones):
        S = self.S
        gv = gW.rearrange("r j -> (r j)").rearrange("(f p k) -> f p k", p=128, k=D)
        uv = uW.rearrange("r j -> (r j)").rearrange("(f p k) -> f p k", p=128, k=D)
        dv = dW.rearrange("r j -> (r j)").rearrange("(d p k) -> d p k", p=128, k=DFF)
        with self.phase() as (sb, ps):
            hT = sb([128, KC, TT], BF16)
            aT = sb([128, FC, TT], BF16)
            nw = sb([128, KC], F32)
            xk = [sb([128, TT], F32) for _ in range(3)]
            sq = [sb([128, TT], F32) for _ in range(2)]
            rstd = sb([128, TT], F32)
            wg = [sb([128, D], BF16) for _ in range(2)]
            wu = [sb([128, D], BF16) for _ in range(2)]
            wd = [sb([128, DFF // 2], BF16) for _ in range(3)]
            sg = [sb([128, TT], F32) for _ in range(2)]
            xo = [sb([128, TT], F32) for _ in range(2)]
            ssq_ps = ps([128, TT])
            g_ps = [ps([128, TT]) for _ in range(2)]
            u_ps = [ps([128, TT]) for _ in range(2)]
            o_ps = [ps([128, TT]) for _ in range(2)]
            S.dma(nw[:], nw_ap[:, :], writes=["nw"])
            ld = 0
            wdi = 0
            for tt in range(NTT):
                tsl = slice(tt * TT, (tt + 1) * TT)
                for kc in range(KC):
                    b = ld % 3
                    ld += 1
                    q = kc % 2
                    S.dma(xk[b][:], xTv[:, kc, tsl], reads=[("xT", tt)], writes=[("xk", b)])
                    S.act(lambda e, b=b, q=q: e.activation(out=sq[q][:], in_=xk[b][:], func=AF.Square),
                          reads=[("xk", b)], writes=[("sq", q)])
                    S.pe(lambda e, q=q, kc=kc: e.matmul(ssq_ps[:], lhsT=ones, rhs=sq[q][:],
                                                       start=(kc == 0), stop=(kc == KC - 1)),
                         reads=[("sq", q), "cft"], writes=["ssq_ps"])
                S.act(lambda e: e.activation(out=rstd[:], in_=ssq_ps[:], func=AF.Sqrt, scale=1.0 / D,
                                             bias=self.eps_ap), reads=["ssq_ps"], writes=["rstd"])
                S.dve(lambda e: e.reciprocal(out=rstd[:], in_=rstd[:]), reads=["rstd"], writes=["rstd"])
                for kc in range(KC):
                    b = ld % 3
                    ld += 1
                    S.dma(xk[b][:], xTv[:, kc, tsl], reads=[("xT", tt)], writes=[("xk", b)])
                    S.dve(lambda e, b=b, kc=kc: e.scalar_tensor_tensor(
                        out=hT[:, kc, :], in0=xk[b][:], scalar=nw[:, kc:kc + 1], in1=rstd[:],
                        op0=ALU.mult, op1=ALU.mult), reads=[("xk", b), "nw", "rstd"], writes=["hT"])
                for fc in range(FC):
                    wb = fc % 2
                    S.dma(wg[wb][:], gv[fc], reads=[self.gk(wnames[0], fc)], writes=[("wg", wb)])
                    S.dma(wu[wb][:], uv[fc], reads=[self.gk(wnames[1], fc)], writes=[("wu", wb)])
                    for kc in range(KC):
                        S.pe(lambda e, wb=wb, kc=kc: e.matmul(g_ps[wb][:], lhsT=wg[wb][:, kc * 128:(kc + 1) * 128],
                                                            rhs=hT[:, kc, :], start=(kc == 0), stop=(kc == KC - 1)),
                             reads=[("wg", wb), "hT"], writes=[("g_ps", wb)])
                    for kc in range(KC):
                        S.pe(lambda e, wb=wb, kc=kc: e.matmul(u_ps[wb][:], lhsT=wu[wb][:, kc * 128:(kc + 1) * 128],
                                                            rhs=hT[:, kc, :], start=(kc == 0), stop=(kc == KC - 1)),
                             reads=[("wu", wb), "hT"], writes=[("u_ps", wb)])
                    S.act(lambda e, wb=wb: e.activation(out=sg[wb][:], in_=g_ps[wb][:], func=AF.Silu),
                          reads=[("g_ps", wb)], writes=[("sg", wb)])
                    S.dve(lambda e, wb=wb, fc=fc: e.tensor_tensor(out=aT[:, fc, :], in0=sg[wb][:], in1=u_ps[wb][:],
                                                                  op=ALU.mult),
                          reads=[("sg", wb), ("u_ps", wb)], writes=["aT"])
                for dc in range(KC):
                    wb = dc % 2
                    b = ld % 3
                    ld += 1
                    S.dma(xk[b][:], xTv[:, dc, tsl], reads=[("xT", tt)], writes=[("xk", b)])
                    for hf in range(2):
                        db = wdi % 3
                        wdi += 1
                        S.dma(wd[db][:], dv[dc][:, hf * (DFF // 2):(hf + 1) * (DFF // 2)],
                              reads=[self.gk(wnames[2], dc)], writes=[("wd", db)])
                        for f2 in range(FC // 2):
                            fc = hf * (FC // 2) + f2
                            S.pe(lambda e, wb=wb, db=db, fc=fc, f2=f2: e.matmul(
                                o_ps[wb][:], lhsT=wd[db][:, f2 * 128:(f2 + 1) * 128],
                                rhs=aT[:, fc, :], start=(fc == 0), stop=(fc == FC - 1)),
                                reads=[("wd", db), "aT"], writes=[("o_ps", wb)])
                    S.dve(lambda e, wb=wb, b=b: e.scalar_tensor_tensor(
                        out=xo[wb][:], in0=o_ps[wb][:], scalar=0.5, in1=xk[b][:], op0=ALU.mult, op1=ALU.add),
                        reads=[("o_ps", wb), ("xk", b)], writes=[("xo", wb)])
                    S.dma(xTv[:, dc, tsl], xo[wb][:], reads=[("xo", wb)], writes=[("xTw", tt, dc)], q="act")
                S.dve(lambda e: e.memset(self.bar_scratch[:, 4:8], 0.0),
                      reads=[("xTw", tt, dc) for dc in range(KC)], writes=[("xT", tt)])

    def norm_setup(self, sb, ps, nw_ap):
        S = self.S
        r = dict(nw=sb([128, KC], F32), xk=[sb([128, TT], F32) for _ in range(2)],
                 sq=[sb([128, TT], F32) for _ in range(2)], rstd=sb([128, TT], F32), ssq=ps([128, TT]),
                 u=self.uid("nrm"), ld=0)
        S.dma(r["nw"][:], nw_ap[:, :], writes=[r["u"] + "nw"])
        return r

    def norm_tile(self, r, xTv, tt, out_fn, out_key):
        S = self.S
        u = r["u"]
        tsl = slice(tt * TT, (tt + 1) * TT)
        xk, sq, rstd, ssq, nw = r["xk"], r["sq"], r["rstd"], r["ssq"], r["nw"]
        for kc in range(KC):
            b = r["ld"] % 2
            r["ld"] += 1
            q = kc % 2
            S.dma(xk[b][:], xTv[:, kc, tsl], reads=[("xT", tt)], writes=[(u, "xk", b)])
            S.act(lambda e, b=b, q=q: e.activation(out=sq[q][:], in_=xk[b][:], func=AF.Square),
                  reads=[(u, "xk", b)], writes=[(u, "sq", q)])
            S.pe(lambda e, q=q, kc=kc: e.matmul(ssq[:], lhsT=self.ones, rhs=sq[q][:],
                                               start=(kc == 0), stop=(kc == KC - 1)),
                 reads=[(u, "sq", q), "cft"], writes=[(u, "ssq")])
        S.act(lambda e: e.activation(out=rstd[:], in_=ssq[:], func=AF.Sqrt, scale=1.0 / D, bias=self.eps_ap),
              reads=[(u, "ssq"), "cft"], writes=[(u, "rstd")])
        S.dve(lambda e: e.reciprocal(out=rstd[:], in_=rstd[:]), reads=[(u, "rstd")], writes=[(u, "rstd")])
        for kc in range(KC):
            b = r["ld"] % 2
            r["ld"] += 1
            S.dma(xk[b][:], xTv[:, kc, tsl], reads=[("xT", tt)], writes=[(u, "xk", b)])
            S.dve(lambda e, b=b, kc=kc: e.scalar_tensor_tensor(
                out=out_fn(kc), in0=xk[b][:], scalar=nw[:, kc:kc + 1], in1=rstd[:],
                op0=ALU.mult, op1=ALU.mult), reads=[(u, "xk", b), u + "nw", (u, "rstd")], writes=[out_key])

    def proj_residual(self, inv, wv, gkey, Kc, xTv):
        S = self.S
        with self.phase() as (sb, ps):
            aT = sb([128, Kc, TT], BF16)
            wd = [sb([128, Kc * 128], BF16) for _ in range(2)]
            xk = [sb([128, TT], F32) for _ in range(2)]
            xo = [sb([128, TT], F32) for _ in range(2)]
            o_ps = [ps([128, TT]) for _ in range(2)]
            for tt in range(NTT):
                tsl = slice(tt * TT, (tt + 1) * TT)
                S.dma(aT[:], inv[:, :, tsl], writes=["aT"])
                for dc in range(KC):
                    wb = dc % 2
                    S.dma(wd[wb][:], wv[dc], reads=[self.gk(gkey, dc)], writes=[("wd", wb)])
                    S.dma(xk[wb][:], xTv[:, dc, tsl], reads=[("xT", tt)], writes=[("xk", wb)], q="act")
                    for kc in range(Kc):
                        S.pe(lambda e, wb=wb, kc=kc: e.matmul(o_ps[wb][:], lhsT=wd[wb][:, kc * 128:(kc + 1) * 128],
                                                            rhs=aT[:, kc, :], start=(kc == 0), stop=(kc == Kc - 1)),
                             reads=[("wd", wb), "aT"], writes=[("o_ps", wb)])
                    S.dve(lambda e, wb=wb: e.tensor_tensor(out=xo[wb][:], in0=o_ps[wb][:], in1=xk[wb][:], op=ALU.add),
                          reads=[("o_ps", wb), ("xk", wb)], writes=[("xo", wb)])
                    S.dma(xTv[:, dc, tsl], xo[wb][:], reads=[("xo", wb)], writes=[("xTw", tt, dc)], q="act")
                S.dve(lambda e: e.memset(self.bar_scratch[:, 4:8], 0.0),
                      reads=[("xTw", tt, dc) for dc in range(KC)], writes=[("xT", tt)])

    def mix1(self, G, mw, xTv):
        S = self.S
        cft = self.cft
        qkvw = G["qkvw"].rearrange("r j -> (r j)").rearrange("(n p k) -> n p k", p=128, k=D)
        wow = G["wow"].rearrange("r j -> (r j)").rearrange("(n p k) -> n p k", p=128, k=D)
        qkvT = self.scratch("qkvT", [48 * 128, L], BF16)
        attnT = self.scratch("attnT", [D, L], BF16)
        V_tm = self.scratch("V_tm", [L, 1024], BF16)
        PiT = cft[:, C_PIT:C_PIT + 128]
        with self.phase() as (sb, ps):
            hT = sb([128, KC, L], BF16)
            nr = self.norm_setup(sb, ps, mw["mn1"])
            qkn = sb([128, 2], F32)
            cos = sb([128, L], F32)
            sin = sb([128, L], F32)
            S.dma(qkn[:], mw["qkn"][:, :], writes=["qkn"])
            S.dma(cos[:], mw["cos"][:, :], writes=["cos"])
            S.dma(sin[:], mw["sin"][:, :], writes=["sin"])
            for tt in range(NTT):
                self.norm_tile(nr, xTv, tt, lambda kc, tt=tt: hT[:, kc, tt * TT:(tt + 1) * TT], "hT")
            w = [sb([128, D], BF16) for _ in range(2)]
            pj = [ps([128, TT]) for _ in range(5)]
            ssq = ps([128, TT])
            px = ps([128, TT])
            xs = [sb([128, TT], F32) for _ in range(2)]
            sq = [sb([128, TT], F32) for _ in range(2)]
            xw = [sb([128, TT], F32) for _ in range(2)]
            t1 = [sb([128, TT], F32) for _ in range(2)]
            t2 = [sb([128, TT], F32) for _ in range(2)]
            rs = [sb([128, TT], F32) for _ in range(2)]
            obt = [sb([128, TT], BF16) for _ in range(2)]
            it = 0
            vstg = [sb([128, 4, 128], BF16) for _ in range(2)]
            for hc in range(48):
                wb = hc % 2
                S.dma(w[wb][:], qkvw[hc], reads=[self.gk("qkvw", hc)], writes=[("w", wb)])
                if hc >= 40:
                    kvh = hc - 40
                    for m4 in range(4):
                        pb = it % 5
                        b = it % 2
                        it += 1
                        for j in range(4):
                            m = m4 * 4 + j
                            for kc in range(KC):
                                S.pe(lambda e, wb=wb, kc=kc, pb=pb, j=j, m=m: e.matmul(
                                    pj[pb][:, j * 128:(j + 1) * 128], lhsT=hT[:, kc, m * 128:(m + 1) * 128],
                                    rhs=w[wb][:, kc * 128:(kc + 1) * 128], start=(kc == 0), stop=(kc == KC - 1)),
                                    reads=[("w", wb), "hT"], writes=[("pj", pb)])
                        S.act(lambda e, b=b, pb=pb: e.activation(out=vstg[b][:].rearrange("p a d -> p (a d)"),
                                                                 in_=pj[pb][:], func=AF.Copy),
                              reads=[("pj", pb)], writes=[("vstg", b)])
                        S.dma(V_tm[m4 * 512:(m4 + 1) * 512, kvh * 128:(kvh + 1) * 128].rearrange(
                            "(a p) d -> p a d", p=128), vstg[b][:], reads=[("vstg", b)], q="act")
                    continue
                for tt in range(NTT):
                    tsl = slice(tt * TT, (tt + 1) * TT)
                    pb = it % 5
                    b = it % 2
                    it += 1
                    for kc in range(KC):
                        S.pe(lambda e, wb=wb, kc=kc, pb=pb, tsl=tsl: e.matmul(
                            pj[pb][:], lhsT=w[wb][:, kc * 128:(kc + 1) * 128], rhs=hT[:, kc, tsl],
                            start=(kc == 0), stop=(kc == KC - 1)), reads=[("w", wb), "hT"], writes=[("pj", pb)])
                    if hc >= 40:
                        S.act(lambda e, b=b, pb=pb: e.activation(out=obt[b][:], in_=pj[pb][:], func=AF.Copy),
                              reads=[("pj", pb)], writes=[("ob", b)])
                    else:
                        col = 0 if hc < 32 else 1
                        scl = (128.0 ** -0.5) if hc < 32 else 1.0
                        S.act(lambda e, b=b, pb=pb: e.activation(out=sq[b][:], in_=pj[pb][:], func=AF.Square),
                              reads=[("pj", pb)], writes=[("sq", b)])
                        S.dve(lambda e, b=b, pb=pb, col=col: e.tensor_scalar(
                            out=xw[b][:], in0=pj[pb][:], scalar1=qkn[:, col:col + 1], scalar2=None, op0=ALU.mult),
                            reads=[("pj", pb), "qkn"], writes=[("xw", b)])
                        S.pe(lambda e, b=b: e.matmul(ssq[:], lhsT=self.ones, rhs=sq[b][:], start=True, stop=True),
                             reads=[("sq", b), "cft"], writes=["ssq"])
                        S.pe(lambda e, b=b: e.matmul(px[:], lhsT=PiT, rhs=xw[b][:], start=True, stop=True),
                             reads=[("xw", b), "cft"], writes=["px"])
                        S.act(lambda e, b=b: e.activation(out=rs[b][:], in_=ssq[:], func=AF.Sqrt, scale=1.0 / 128,
                                                          bias=self.eps_ap), reads=["ssq", "cft"], writes=[("rs", b)])
                        S.dve(lambda e, b=b: e.reciprocal(out=rs[b][:], in_=rs[b][:]), reads=[("rs", b)],
                              writes=[("rs", b)])
                        S.dve(lambda e, b=b, tsl=tsl: e.tensor_tensor(out=t1[b][:], in0=xw[b][:], in1=cos[:, tsl],
                                                                      op=ALU.mult),
                              reads=[("xw", b), "cos"], writes=[("t1", b)])
                        S.dve(lambda e, b=b, tsl=tsl: e.tensor_tensor(out=t2[b][:], in0=px[:], in1=sin[:, tsl],
                                                                      op=ALU.mult),
                              reads=["px", "sin"], writes=[("t2", b)])
                        S.dve(lambda e, b=b: e.tensor_tensor(out=t1[b][:], in0=t1[b][:], in1=t2[b][:], op=ALU.add),
                               reads=[("t1", b), ("t2", b)], writes=[("t1", b)])
                        S.dve(lambda e, b=b, scl=scl: e.scalar_tensor_tensor(
                            out=obt[b][:], in0=t1[b][:], scalar=scl, in1=rs[b][:], op0=ALU.mult, op1=ALU.mult),
                            reads=[("t1", b), ("rs", b)], writes=[("ob", b)])
                    S.dma(qkvT[hc * 128:(hc + 1) * 128, tsl], obt[b][:], reads=[("ob", b)], writes=[], q="act")
        if self.cfg.get('m1_stop') == 'A':
            return
        with self.phase() as (sb, ps):
            onesb = sb([128, 128], BF16)
            identb = sb([128, 128], BF16)
            S.dve(lambda e: e.tensor_copy(out=onesb[:], in_=self.ones), reads=["cft"], writes=["onesb"])
            S.dve(lambda e: e.tensor_copy(out=identb[:], in_=self.ident), reads=["cft"], writes=["identb"])
            kT = [sb([128, L], BF16) for _ in range(2)]
            vT = [sb([128, L], BF16) for _ in range(2)]
            V = [sb([128, 16, 128], BF16) for _ in range(2)]
            qT = [sb([128, L], BF16) for _ in range(2)]
            pT = [sb([128, TT], BF16) for _ in range(4)]
            rinv = [sb([128, TT], F32) for _ in range(2)]
            oT = [sb([128, TT], BF16) for _ in range(2)]
            s_ps = [ps([128, TT]) for _ in range(3)]
            o_ps = [ps([128, TT]) for _ in range(2)]
            r_ps = [ps([128, TT]) for _ in range(2)]
            si = 0
            pi = 0
            oi = 0
            qi = 0
            for kvh in range(8):
                kb = kvh % 2
                S.dma(kT[kb][:], qkvT[(32 + kvh) * 128:(33 + kvh) * 128, :], writes=[("kT", kb)])
                S.dma(V[kb][:], V_tm[:, kvh * 128:(kvh + 1) * 128].rearrange("(a p) d -> p a d", p=128),
                      writes=[("V", kb)])
                for hq in range(4):
                    h = kvh * 4 + hq
                    qb = qi % 2
                    qi += 1
                    S.dma(qT[qb][:], qkvT[h * 128:(h + 1) * 128, :], writes=[("qT", qb)])
                    for qt in range(NTT):
                        tsl = slice(qt * TT, (qt + 1) * TT)
                        ob = oi % 2
                        oi += 1
                        for sc in range(16):
                            sb_ = si % 3
                            si += 1
                            pb = pi % 4
                            pi += 1
                            S.pe(lambda e, kb=kb, qb=qb, sc=sc, sb_=sb_, tsl=tsl: e.matmul(
                                s_ps[sb_][:], lhsT=kT[kb][:, sc * 128:(sc + 1) * 128], rhs=qT[qb][:, tsl],
                                start=True, stop=True), reads=[("kT", kb), ("qT", qb)], writes=[("s_ps", sb_)])
                            S.act(lambda e, sb_=sb_, pb=pb: e.activation(out=pT[pb][:], in_=s_ps[sb_][:], func=AF.Exp),
                                  reads=[("s_ps", sb_)], writes=[("pT", pb)])
                            S.pe(lambda e, kb=kb, sc=sc, pb=pb, ob=ob: e.matmul(
                                o_ps[ob][:], lhsT=V[kb][:, sc, :], rhs=pT[pb][:], start=(sc == 0), stop=(sc == 15)),
                                reads=[("V", kb), ("pT", pb)], writes=[("o_ps", ob)])
                            S.pe(lambda e, pb=pb, ob=ob, sc=sc: e.matmul(
                                r_ps[ob][:], lhsT=onesb[:], rhs=pT[pb][:], start=(sc == 0), stop=(sc == 15)),
                                reads=["onesb", ("pT", pb)], writes=[("r_ps", ob)])
                        S.act(lambda e, ob=ob: e.activation(out=rinv[ob][:], in_=r_ps[ob][:], func=AF.Copy),
                              reads=[("r_ps", ob)], writes=[("rinv", ob)])
                        S.dve(lambda e, ob=ob: e.reciprocal(out=rinv[ob][:], in_=rinv[ob][:]),
                              reads=[("rinv", ob)], writes=[("rinv", ob)])
                        S.dve(lambda e, ob=ob: e.tensor_tensor(out=oT[ob][:], in0=o_ps[ob][:], in1=rinv[ob][:],
                                                               op=ALU.mult),
                              reads=[("o_ps", ob), ("rinv", ob)], writes=[("oT", ob)])
                        S.dma(attnT[h * 128:(h + 1) * 128, tsl], oT[ob][:], reads=[("oT", ob)], writes=[], q="act")
        if self.cfg.get('m1_stop') == 'B':
            return
        self.proj_residual(attnT.rearrange("(k p) l -> p k l", p=128), wow, "wow", KC, xTv)

    def mix0(self, G, mw, xTv):
        S = self.S
        cft = self.cft
        ident, ones = self.ident, self.ones
        flat = lambda a: a.rearrange("r j -> (r j)")
        inw = flat(G["inw"]).rearrange("(n p k) -> n p k", p=128, k=D)
        dtw = flat(G["dtw"]).rearrange("(p k) -> p k", p=128)
        outw = flat(G["outw"]).rearrange("(n p k) -> n p k", p=128, k=8192)
        CLv = flat(G["dftc"]).rearrange("(p a l) -> p a l", p=128, l=L)
        SLv = flat(G["dfts"]).rearrange("(p a l) -> p a l", p=128, l=L)
        uT_scr = self.scratch("uT_scr", [2048, L], BF16)
        szT_scr = self.scratch("szT_scr", [6144, L], F32)
        x_tm = self.scratch("x_tm", [L, 6144], F32)
        B_tm = self.scratch("B_tm", [L, 1024], BF16)
        BT_scr = self.scratch("BT_scr", [1024, L], BF16)
        CT_scr = self.scratch("CT_scr", [1024, L], BF16)
        dt_scr = self.scratch("dt_scr", [L, 192], F32)
        mixin = self.scratch("mixin", [8192, L], BF16)
        FSCALE = float(1.0 / np.sqrt(2048.0 * 256.0))
        with self.phase() as (sb, ps):
            hT = sb([128, KC, L], BF16)
            nr = self.norm_setup(sb, ps, mw["mn0"])
            for tt in range(NTT):
                self.norm_tile(nr, xTv, tt, lambda kc, tt=tt: hT[:, kc, tt * TT:(tt + 1) * TT], "hT")
            cw = sb([128, 64, 5], F32)
            cb = sb([128, 64], F32)
            dtb = sb([128, 192], F32)
            S.dma(cw[:], mw["cw"][:, :, :], writes=["cw"])
            S.dma(cb[:], mw["cb"][:, :], writes=["cb"])
            S.dma(dtb[:], mw["dtb"][:, :], writes=["dtb"])
            w = [sb([128, D], BF16) for _ in range(2)]
            wdt = sb([128, 6144], BF16)
            pj = [ps([128, TT]) for _ in range(5)]
            tp = [ps([128, 512]) for _ in range(2)]
            cin = sb([128, L + 4], F32)
            xc = sb([128, L], F32)
            xcb = sb([128, L], BF16)
            stg = [sb([128, TT], F32) for _ in range(2)]
            stgb = [sb([128, TT], BF16) for _ in range(2)]
            tst = [sb([128, 4, 128], F32) for _ in range(2)]
            tstb = [sb([128, 4, 128], BF16) for _ in range(2)]
            S.dve(lambda e: e.memset(cin[:], 0.0), writes=["cin"])
            it = 0
            ti = 0
            for nci in range(128):
                wb = nci % 2
                S.dma(w[wb][:], inw[nci], reads=[self.gk("inw", nci)], writes=[("w", wb)])
                for tt in range(NTT):
                    tsl = slice(tt * TT, (tt + 1) * TT)
                    pb = it % 5
                    b = it % 2
                    it += 1
                    for kc in range(KC):
                        S.pe(lambda e, wb=wb, kc=kc, pb=pb, tsl=tsl: e.matmul(
                            pj[pb][:], lhsT=w[wb][:, kc * 128:(kc + 1) * 128], rhs=hT[:, kc, tsl],
                            start=(kc == 0), stop=(kc == KC - 1)), reads=[("w", wb), "hT"], writes=[("pj", pb)])
                    if nci < 16:
                        S.act(lambda e, b=b, pb=pb: e.activation(out=stgb[b][:], in_=pj[pb][:], func=AF.Copy),
                              reads=[("pj", pb)], writes=[("stgb", b)])
                        S.dma(uT_scr[nci * 128:(nci + 1) * 128, tsl], stgb[b][:], reads=[("stgb", b)], q="act")
                    elif nci < 64:
                        S.act(lambda e, b=b, pb=pb: e.activation(out=stg[b][:], in_=pj[pb][:], func=AF.Silu),
                              reads=[("pj", pb)], writes=[("stg", b)])
                        S.dma(szT_scr[(nci - 16) * 128:(nci - 15) * 128, tsl], stg[b][:], reads=[("stg", b)], q="act")
                    else:
                        S.act(lambda e, pb=pb, tt=tt: e.activation(out=cin[:, 2 + tt * TT:2 + (tt + 1) * TT],
                                                                   in_=pj[pb][:], func=AF.Copy),
                              reads=[("pj", pb)], writes=["cin"])
                if nci >= 64:
                    ch = nci - 64
                    S.dve(lambda e, ch=ch: e.tensor_scalar(out=xc[:], in0=cin[:, 0:L], scalar1=cw[:, ch, 0:1],
                                                           scalar2=None, op0=ALU.mult),
                          reads=["cin", "cw"], writes=["xc"])
                    for j in range(1, 5):
                        S.dve(lambda e, ch=ch, j=j: e.scalar_tensor_tensor(
                            out=xc[:], in0=cin[:, j:j + L], scalar=cw[:, ch, j:j + 1], in1=xc[:],
                            op0=ALU.mult, op1=ALU.add), reads=["cin", "cw", "xc"], writes=["xc"])
                    S.act(lambda e, ch=ch: e.activation(out=xc[:], in_=xc[:], func=AF.Silu, bias=cb[:, ch:ch + 1]),
                          reads=["xc", "cb"], writes=["xc"])
                    if ch < 56:
                        for m4 in range(4):
                            tb = ti % 2
                            ti += 1
                            for j in range(4):
                                lt = m4 * 4 + j
                                S.pe(lambda e, tb=tb, j=j, lt=lt: e.transpose(
                                    out=tp[tb][:, j * 128:(j + 1) * 128], in_=xc[:, lt * 128:(lt + 1) * 128],
                                    identity=ident), reads=["xc", "cft"], writes=[("tp", tb)])
                            rows = slice(m4 * 512, (m4 + 1) * 512)
                            if ch < 48:
                                S.act(lambda e, tb=tb: e.activation(out=tst[tb][:].rearrange("p a c -> p (a c)"),
                                                                    in_=tp[tb][:], func=AF.Copy),
                                      reads=[("tp", tb)], writes=[("tst", tb)])
                                S.dma(x_tm[rows, ch * 128:(ch + 1) * 128].rearrange("(a p) c -> p a c", p=128),
                                      tst[tb][:], reads=[("tst", tb)], q="act")
                            else:
                                gq = ch - 48
                                S.act(lambda e, tb=tb: e.activation(out=tstb[tb][:].rearrange("p a c -> p (a c)"),
                                                                    in_=tp[tb][:], func=AF.Copy),
                                      reads=[("tp", tb)], writes=[("tstb", tb)])
                                S.dma(B_tm[rows, gq * 128:(gq + 1) * 128].rearrange("(a p) c -> p a c", p=128),
                                      tstb[tb][:], reads=[("tstb", tb)], q="act")
                    if ch >= 48:
                        S.dve(lambda e: e.tensor_copy(out=xcb[:], in_=xc[:]), reads=["xc"], writes=["xcb"])
                        dst = BT_scr if ch < 56 else CT_scr
                        gq = (ch - 48) % 8
                        S.dma(dst[gq * 128:(gq + 1) * 128, :], xcb[:], reads=["xcb"], q="act")
            S.dma(wdt[:], dtw, reads=[self.gk("dtw", 0)], writes=["wdt"])
            for m in range(16):
                pb = it % 5
                b = it % 2
                it += 1
                for kc in range(KC):
                    S.pe(lambda e, kc=kc, pb=pb, m=m: e.matmul(
                        pj[pb][:, 0:192], lhsT=hT[:, kc, m * 128:(m + 1) * 128], rhs=wdt[:, kc * 192:(kc + 1) * 192],
                        start=(kc == 0), stop=(kc == KC - 1)), reads=["wdt", "hT"], writes=[("pj", pb)])
                tq = stg[b][:, 0:192]
                aq = stg[b][:, 192:384]
                S.dve(lambda e, tq=tq, pb=pb: e.tensor_tensor(out=tq, in0=pj[pb][:, 0:192], in1=dtb[:], op=ALU.add),
                      reads=[("pj", pb), "dtb"], writes=[("stg", b)])
                S.act(lambda e, tq=tq, aq=aq: e.activation(out=aq, in_=tq, func=AF.Abs),
                      reads=[("stg", b)], writes=[("stg", b)])
                S.act(lambda e, aq=aq: e.activation(out=aq, in_=aq, func=AF.Exp, scale=-1.0),
                      reads=[("stg", b)], writes=[("stg", b)])
                S.act(lambda e, aq=aq: e.activation(out=aq, in_=aq, func=AF.Ln, bias=self.one_col),
                      reads=[("stg", b), "cft"], writes=[("stg", b)])
                S.dve(lambda e, tq=tq, aq=aq: e.scalar_tensor_tensor(out=tq, in0=tq, scalar=0.0, in1=aq,
                                                                     op0=ALU.max, op1=ALU.add),
                      reads=[("stg", b)], writes=[("stg", b)])
                S.dma(dt_scr[m * 128:(m + 1) * 128, :], tq, reads=[("stg", b)], q="act")
        with self.phase() as (sb, ps):
            CL = sb([128, 16, L], BF16)
            SL = sb([128, 16, L], BF16)
            cdsd = sb([128, 2, 512], BF16)
            S.dma(CL[:], CLv, reads=[self.gk("dftc", 0)], writes=["CL"])
            S.dma(SL[:], SLv, reads=[self.gk("dfts", 0)], writes=["SL"])
            S.dma(cdsd[:], mw["cdsd"][:, :, :], writes=["cdsd"])
            uT = [sb([128, 2, L], BF16) for _ in range(2)]
            PQ = [sb([128, 16, 512], BF16) for _ in range(2)]
            stgbB = [sb([128, TT], BF16) for _ in range(2)]
            pq_ps = [ps([128, 512]) for _ in range(2)]
            y_ps = [ps([128, 512]) for _ in range(2)]
            yi = 0
            for g in range(8):
                gb = g % 2
                S.dma(uT[gb][:], uT_scr[g * 256:(g + 1) * 256, :].rearrange("(c p) l -> p c l", p=128),
                      writes=[("uT", gb)])
                for lt in range(16):
                    pb = lt % 2
                    for dch in range(2):
                        S.pe(lambda e, gb=gb, dch=dch, lt=lt, pb=pb: e.matmul(
                            pq_ps[pb][:], lhsT=uT[gb][:, dch, lt * 128:(lt + 1) * 128], rhs=cdsd[:, dch, :],
                            start=(dch == 0), stop=(dch == 1)), reads=[("uT", gb), "cdsd"], writes=[("pq_ps", pb)])
                    S.act(lambda e, gb=gb, lt=lt, pb=pb: e.activation(out=PQ[gb][:, lt, 0:256], in_=pq_ps[pb][:, 0:256],
                                                                      func=AF.Copy),
                          reads=[("pq_ps", pb)], writes=[("PQ", gb)])
                    S.dve(lambda e, gb=gb, lt=lt, pb=pb: e.tensor_scalar(
                        out=PQ[gb][:, lt, 256:512], in0=pq_ps[pb][:, 256:512], scalar1=-1.0, scalar2=None,
                        op0=ALU.mult), reads=[("pq_ps", pb)], writes=[("PQ", gb)])
                for dch in range(2):
                    for l4 in range(4):
                        yb = yi % 2
                        yi += 1
                        lsl = slice(l4 * 512, (l4 + 1) * 512)
                        for lt in range(16):
                            S.pe(lambda e, gb=gb, dch=dch, lt=lt, yb=yb, lsl=lsl: e.matmul(
                                y_ps[yb][:], lhsT=PQ[gb][:, lt, dch * 128:(dch + 1) * 128], rhs=CL[:, lt, lsl],
                                start=(lt == 0), stop=False), reads=[("PQ", gb), "CL"], writes=[("y_ps", yb)])
                            S.pe(lambda e, gb=gb, dch=dch, lt=lt, yb=yb, lsl=lsl: e.matmul(
                                y_ps[yb][:], lhsT=PQ[gb][:, lt, 256 + dch * 128:256 + (dch + 1) * 128],
                                rhs=SL[:, lt, lsl], start=False, stop=(lt == 15)),
                                reads=[("PQ", gb), "SL"], writes=[("y_ps", yb)])
                        S.dve(lambda e, yb=yb: e.tensor_scalar(out=stgbB[yb][:], in0=y_ps[yb][:], scalar1=FSCALE,
                                                               scalar2=None, op0=ALU.mult),
                              reads=[("y_ps", yb)], writes=[("stgbB", yb)])
                        r0 = g * 256 + dch * 128
                        S.dma(mixin[r0:r0 + 128, lsl], stgbB[yb][:], reads=[("stgbB", yb)], q="act")
        with self.phase() as (sb, ps):
            Minc = cft[:, C_MINC:C_MINC + 128]
            Mdec = cft[:, C_MDEC:C_MDEC + 128]
            Sgt = cft[:, C_SGT:C_SGT + 128]
            Slt = cft[:, C_SLT:C_SLT + 128]
            Abc = sb([128, 192], F32)
            Dbc = sb([128, 96], F32)
            gnw = sb([128, 48], F32)
            tmpc = sb([128, 192], F32)
            S.dma(Abc[:], mw["alog"][:, :], writes=["Abc"])
            S.dma(Dbc[:], mw["dsk"][:, :], writes=["Dbc"])
            S.dma(gnw[:], mw["gnw"][:, :], writes=["gnw"])
            S.act(lambda e: e.activation(out=Abc[:], in_=Abc[:], func=AF.Exp), reads=["Abc"], writes=["Abc"])
            S.dve(lambda e: e.tensor_scalar(out=Abc[:], in0=Abc[:], scalar1=-1.0, scalar2=None, op0=ALU.mult),
                  reads=["Abc"], writes=["Abc"])
            dt_all = sb([128, 16, 192], F32)
            dtA = sb([128, 16, 192], F32)
            ecum = sb([128, 16, 192], F32)
            wdec = sb([128, 16, 192], F32)
            cdec = sb([128, 16, 192], F32)
            S.dma(dt_all[:], dt_scr.rearrange("(c p) h -> p c h", p=128), writes=["dt_all"])
            S.dve(lambda e: e.tensor_tensor(out=dtA[:], in0=dt_all[:],
                                            in1=Abc[:].unsqueeze(1).to_broadcast([128, 16, 192]), op=ALU.mult),
                  reads=["dt_all", "Abc"], writes=["dtA"])
            P3 = ps([128, 1536])
            PA = ps([128, 1024])
            PB = ps([128, 1024])
            PC = ps([128, 512])
            for c in range(16):
                S.pe(lambda e, c=c: e.matmul(PC[:, 0:96], lhsT=Minc, rhs=dtA[:, c, 0:96], start=True, stop=True),
                     reads=["dtA", "cft"], writes=["PC"])
                S.pe(lambda e, c=c: e.matmul(PC[:, 96:192], lhsT=Mdec, rhs=dtA[:, c, 96:192], start=True, stop=True),
                     reads=["dtA", "cft"], writes=["PC"])
                S.pe(lambda e, c=c: e.matmul(PC[:, 192:384], lhsT=ones, rhs=dtA[:, c, :], start=True, stop=True),
                     reads=["dtA", "cft"], writes=["PC"])
                S.act(lambda e, c=c: e.activation(out=ecum[:, c, :], in_=PC[:, 0:192], func=AF.Exp),
                      reads=["PC"], writes=["ecum"])
                S.act(lambda e, c=c: e.activation(out=cdec[:, c, :], in_=PC[:, 192:384], func=AF.Exp),
                      reads=["PC"], writes=["cdec"])
                S.act(lambda e: e.activation(out=tmpc[:], in_=PC[:, 0:192], func=AF.Copy),
                      reads=["PC"], writes=["tmpc"])
                S.dve(lambda e, c=c: e.tensor_tensor(out=wdec[:, c, :], in0=PC[:, 192:384], in1=tmpc[:],
                                                     op=ALU.subtract), reads=["PC", "tmpc"], writes=["wdec"])
                S.act(lambda e, c=c: e.activation(out=wdec[:, c, :], in_=wdec[:, c, :], func=AF.Exp),
                      reads=["wdec"], writes=["wdec"])
                S.dve(lambda e, c=c: e.tensor_tensor(out=wdec[:, c, :], in0=wdec[:, c, :], in1=dt_all[:, c, :],
                                                     op=ALU.mult), reads=["wdec", "dt_all"], writes=["wdec"])
            BT = [sb([128, L], BF16) for _ in range(2)]
            CT = [sb([128, L], BF16) for _ in range(2)]
            xg = [sb([128, 12, 64], F32) for _ in range(2)]
            Btm = [sb([128, 128], BF16) for _ in range(2)]
            CBm = [sb([128, 128], F32) for _ in range(2)]
            Dm = sb([128, 12, 128], F32)
            E = sb([128, 12, 128], F32)
            M = [sb([128, 12, 128], BF16) for _ in range(2)]
            xs = [sb([128, 12, 64], BF16) for _ in range(2)]
            xsd = [sb([128, 12, 64], BF16) for _ in range(2)]
            prev_f = sb([128, 12, 64], F32)
            prev_b = sb([128, 768], BF16)
            tmp = [sb([128, 12, 64], F32) for _ in range(2)]
            y_acc = sb([128, 16, 768], F32)
            szt = sb([128, 6, TT], F32)
            yg = sb([128, 6, TT], F32)
            sq = sb([128, TT], F32)
            rstd = sb([128, TT], F32)
            stgbC = [sb([128, TT], BF16) for _ in range(2)]
            k3 = lambda a: a.rearrange("p (k d) -> p k d", d=64)
            f2 = lambda a: a.rearrange("p k d -> p (k d)")
            it = 0
            si = 0
            for g in range(8):
                gb = g % 2
                S.dma(BT[gb][:], BT_scr[g * 128:(g + 1) * 128, :], writes=[("BT", gb)])
                S.dma(CT[gb][:], CT_scr[g * 128:(g + 1) * 128, :], writes=[("CT", gb)])
                for dr in range(2):
                    S.dve(lambda e: e.memset(prev_f[:], 0.0), writes=["prev_f"])
                    S.dve(lambda e: e.memset(prev_b[:], 0.0), writes=["prev_b"])
                    mk = Minc if dr == 0 else Mdec
                    sx = Sgt if dr == 0 else Slt
                    col0 = dr * 96 + g * 12
                    for ci in range(16):
                        c = ci if dr == 0 else 15 - ci
                        b = it % 2
                        it += 1
                        csl = slice(c * 128, (c + 1) * 128)
                        S.dma(f2(xg[b][:]), x_tm[csl, g * 768:(g + 1) * 768], writes=[("xg", b)])
                        S.dma(Btm[b][:], B_tm[csl, g * 128:(g + 1) * 128], writes=[("Btm", b)])
                        S.pe(lambda e, gb=gb, csl=csl: e.matmul(PC[:, 0:128], lhsT=BT[gb][:, csl], rhs=CT[gb][:, csl],
                                                               start=True, stop=True),
                             reads=[("BT", gb), ("CT", gb)], writes=["PC"])
                        S.dve(lambda e, b=b, mk=mk: e.tensor_tensor(out=CBm[b][:], in0=PC[:, 0:128], in1=mk,
                                                                    op=ALU.mult),
                              reads=["PC", "cft"], writes=[("CBm", b)])
                        S.dve(lambda e, mk=mk, c=c, col0=col0: e.tensor_tensor(
                            out=Dm[:], in0=mk.unsqueeze(1).to_broadcast([128, 12, 128]),
                            in1=dtA[:, c, col0:col0 + 12].unsqueeze(2).to_broadcast([128, 12, 128]), op=ALU.mult),
                            reads=["cft", "dtA"], writes=["Dm"])
                        Dmf = Dm[:].rearrange("p k t -> p (k t)")
                        for j in range(3):
                            S.pe(lambda e, sx=sx, j=j, Dmf=Dmf: e.matmul(
                                P3[:, j * 512:(j + 1) * 512], lhsT=sx, rhs=Dmf[:, j * 512:(j + 1) * 512],
                                start=True, stop=True), reads=["Dm", "cft"], writes=["P3"])
                        S.act(lambda e: e.activation(out=E[:].rearrange("p k t -> p (k t)"), in_=P3[:], func=AF.Exp),
                              reads=["P3"], writes=["E"])
                        S.dve(lambda e, b=b: e.tensor_tensor(
                            out=M[b][:], in0=E[:], in1=CBm[b][:].unsqueeze(1).to_broadcast([128, 12, 128]),
                            op=ALU.mult), reads=["E", ("CBm", b)], writes=[("M", b)])
                        S.dve(lambda e, b=b, c=c, col0=col0: e.tensor_tensor(
                            out=xs[b][:], in0=xg[b][:],
                            in1=dt_all[:, c, col0:col0 + 12].unsqueeze(2).to_broadcast([128, 12, 64]), op=ALU.mult),
                            reads=[("xg", b), "dt_all"], writes=[("xs", b)])
                        S.dve(lambda e, b=b, c=c, col0=col0: e.tensor_tensor(
                            out=xsd[b][:], in0=xg[b][:],
                            in1=wdec[:, c, col0:col0 + 12].unsqueeze(2).to_broadcast([128, 12, 64]), op=ALU.mult),
                            reads=[("xg", b), "wdec"], writes=[("xsd", b)])
                        for k in range(12):
                            S.pe(lambda e, b=b, k=k: e.matmul(PA[:, k * 64:(k + 1) * 64], lhsT=M[b][:, k, :],
                                                             rhs=xs[b][:, k, :], start=True, stop=True),
                                 reads=[("M", b), ("xs", b)], writes=["PA"])
                        S.pe(lambda e, gb=gb, csl=csl: e.matmul(PB[:, 0:512], lhsT=CT[gb][:, csl], rhs=prev_b[:, 0:512],
                                                               start=True, stop=True),
                             reads=[("CT", gb), "prev_b"], writes=["PB"])
                        S.pe(lambda e, gb=gb, csl=csl: e.matmul(PB[:, 512:768], lhsT=CT[gb][:, csl],
                                                               rhs=prev_b[:, 512:768], start=True, stop=True),
                             reads=[("CT", gb), "prev_b"], writes=["PB"])
                        if dr == 0:
                            S.dve(lambda e, b=b, c=c, g=g: e.tensor_tensor(
                                out=k3(y_acc[:, c, :]), in0=xg[b][:],
                                in1=Dbc[:, g * 12:(g + 1) * 12].unsqueeze(2).to_broadcast([128, 12, 64]),
                                op=ALU.mult), reads=[("xg", b), "Dbc"], writes=[("yacc", c)])
                        S.dve(lambda e, b=b, c=c, col0=col0: e.tensor_tensor(
                            out=tmp[b][:], in0=k3(PB[:, 0:768]),
                            in1=ecum[:, c, col0:col0 + 12].unsqueeze(2).to_broadcast([128, 12, 64]), op=ALU.mult),
                            reads=["PB", "ecum"], writes=[("tmp", b)])
                        S.dve(lambda e, b=b: e.tensor_tensor(out=tmp[b][:], in0=tmp[b][:], in1=k3(PA[:, 0:768]),
                                                             op=ALU.add),
                              reads=[("tmp", b), "PA"], writes=[("tmp", b)])
                        S.dve(lambda e, b=b, c=c: e.tensor_tensor(out=y_acc[:, c, :], in0=y_acc[:, c, :],
                                                                   in1=f2(tmp[b][:]), op=ALU.add),
                               reads=[("tmp", b), ("yacc", c)], writes=[("yacc", c)])
                        S.pe(lambda e, b=b: e.matmul(PB[:, 0:512], lhsT=Btm[b][:], rhs=f2(xsd[b][:])[:, 0:512],
                                                     start=True, stop=True),
                             reads=[("Btm", b), ("xsd", b)], writes=["PB"])
                        S.pe(lambda e, b=b: e.matmul(PB[:, 512:768], lhsT=Btm[b][:], rhs=f2(xsd[b][:])[:, 512:768],
                                                     start=True, stop=True),
                             reads=[("Btm", b), ("xsd", b)], writes=["PB"])
                        S.dve(lambda e, c=c, col0=col0: e.tensor_tensor(
                            out=prev_f[:], in0=prev_f[:],
                            in1=cdec[:, c, col0:col0 + 12].unsqueeze(2).to_broadcast([128, 12, 64]), op=ALU.mult),
                            reads=["prev_f", "cdec"], writes=["prev_f"])
                        S.dve(lambda e: e.tensor_tensor(out=prev_f[:], in0=prev_f[:], in1=k3(PB[:, 0:768]),
                                                        op=ALU.add), reads=["prev_f", "PB"], writes=["prev_f"])
                        S.act(lambda e: e.activation(out=prev_b[:], in_=f2(prev_f[:]), func=AF.Copy),
                              reads=["prev_f"], writes=["prev_b"])
                for tt in range(NTT):
                    tsl = slice(tt * TT, (tt + 1) * TT)
                    S.dma(szt[:], szT_scr[g * 768:(g + 1) * 768, tsl].rearrange("(j p) l -> p j l", p=128),
                          writes=["szt"])
                    for j in range(6):
                        for cc in range(4):
                            c = tt * 4 + cc
                            S.pe(lambda e, c=c, cc=cc, j=j: e.transpose(
                                out=P3[:, cc * 128:(cc + 1) * 128], in_=y_acc[:, c, j * 128:(j + 1) * 128],
                                identity=ident), reads=[("yacc", c), "cft"], writes=["P3"])
                        S.dve(lambda e, j=j: e.tensor_tensor(out=yg[:, j, :], in0=P3[:, 0:512], in1=szt[:, j, :],
                                                             op=ALU.mult), reads=["P3", "szt"], writes=["yg"])
                        S.act(lambda e, j=j: e.activation(out=sq[:], in_=yg[:, j, :], func=AF.Square),
                              reads=["yg"], writes=["sq"])
                        S.pe(lambda e, j=j: e.matmul(PC[:], lhsT=ones, rhs=sq[:], start=(j == 0), stop=(j == 5)),
                             reads=["sq", "cft"], writes=["PC"])
                    S.act(lambda e: e.activation(out=rstd[:], in_=PC[:], func=AF.Sqrt, scale=1.0 / 768,
                                                 bias=self.eps_ap), reads=["PC", "cft"], writes=["rstd"])
                    S.dve(lambda e: e.reciprocal(out=rstd[:], in_=rstd[:]), reads=["rstd"], writes=["rstd"])
                    for j in range(6):
                        sbb = si % 2
                        si += 1
                        S.dve(lambda e, j=j, sbb=sbb, g=g: e.scalar_tensor_tensor(
                            out=stgbC[sbb][:], in0=yg[:, j, :], scalar=gnw[:, g * 6 + j:g * 6 + j + 1], in1=rstd[:],
                            op0=ALU.mult, op1=ALU.mult), reads=["yg", "gnw", "rstd"], writes=[("stgbC", sbb)])
                        r0 = 2048 + g * 768 + j * 128
                        S.dma(mixin[r0:r0 + 128, tsl], stgbC[sbb][:], reads=[("stgbC", sbb)], q="act")
        self.proj_residual(mixin.rearrange("(k p) l -> p k l", p=128), outw, "outw", 64, xTv)


def _tile_stationary(W, kchunks, nchunks):
    K, N = W.shape
    t = W.reshape(kchunks, 128, nchunks, 128).transpose(2, 1, 0, 3)
    return np.ascontiguousarray(t).reshape(nsh(), 128, -1)


DEFAULT_CFG = dict(stages=["ffn00", "mix0", "ffn01", "ffn10", "mix1", "ffn11"], final_norm=True)


def _consts():
    cf = np.zeros((128, NCF), np.float32)
    cf[:, 0:128] = np.eye(128, dtype=np.float32)
    cf[:, 128:256] = 1.0
    cf[:, 256] = EPS
    r = np.arange(128)[:, None]
    c = np.arange(128)[None, :]
    cf[:, C_MINC:C_MINC + 128] = (r <= c)
    cf[:, C_MDEC:C_MDEC + 128] = (r >= c)
    cf[:, C_SGT:C_SGT + 128] = (r > c)
    cf[:, C_SLT:C_SLT + 128] = (r < c)
    pit = np.zeros((128, 128), np.float32)
    for dp in range(128):
        if dp % 64 < 32:
            pit[dp + 32, dp] = -1.0
        else:
            pit[dp - 32, dp] = 1.0
    cf[:, C_PIT:C_PIT + 128] = pit
    return cf


def _rope_tables():
    l = np.arange(L)
    pos = np.stack([l // 64, l % 64], 0).astype(np.float32)
    inv = (10000.0 ** (-(np.arange(0, 64, 2, dtype=np.float32) / 64.0))).astype(np.float32)
    ang = pos[:, None, :] * inv[None, :, None]
    ang = np.concatenate([ang, ang], axis=1).reshape(128, L)
    return np.cos(ang).astype(np.float32), np.sin(ang).astype(np.float32)


def _dft_tables():
    l = np.arange(L, dtype=np.int64)
    m = (l[:, None] * l[None, :]) % L
    a = 2.0 * np.pi * m.astype(np.float64) / L
    cl = np.cos(a).astype(np.float32)
    sl = np.sin(a).astype(np.float32)
    lay = lambda t: np.ascontiguousarray(t.reshape(16, 128, L).transpose(1, 0, 2)).reshape(nsh(), 128, -1)
    d = np.arange(256, dtype=np.int64)
    md = (d[:, None] * d[None, :]) % 256
    ad = 2.0 * np.pi * md.astype(np.float64) / 256
    cdsd = np.concatenate([np.cos(ad), np.sin(ad)], axis=1).astype(np.float32)
    cdsd = np.ascontiguousarray(cdsd.reshape(2, 128, 512).transpose(1, 0, 2)).astype(ml_dtypes.bfloat16)
    return lay(cl), lay(sl), cdsd


def _pc(v, n):
    return np.ascontiguousarray(np.asarray(v, np.float32).reshape(n, 128).T)


def _rep(v):
    v = np.asarray(v, np.float32).reshape(1, -1)
    return np.ascontiguousarray(np.broadcast_to(v, (128, v.shape[1])))


def kernel(cfg=None, **inp):
    cfg = dict(DEFAULT_CFG if cfg is None else cfg)
    stages = cfg["stages"]
    prog = Prog(cfg)
    nc = prog.build()
    x = np.asarray(inp["x"], dtype=np.float32)
    shared = {"cf32": _consts()}
    per_core = [dict() for _ in range(NCORE)]

    def put(name, arr8):
        for c in range(NCORE):
            per_core[c][name] = arr8[c % nsh()]

    for t in stages:
        if t.startswith("ffn"):
            i, s = int(t[3]), int(t[4])
            put(f"ffn{i}{s}g", _tile_stationary(np.asarray(inp["ffn_w_gate"][i, s]), KC, FC))
            put(f"ffn{i}{s}u", _tile_stationary(np.asarray(inp["ffn_w_up"][i, s]), KC, FC))
            put(f"ffn{i}{s}d", _tile_stationary(np.asarray(inp["ffn_w_down"][i, s]), FC, KC))
            shared[f"ffn{i}{s}n"] = _pc(inp["ffn_norm"][i, s], KC)
        elif t == "mix0":
            W = np.asarray(inp["hyb_in_proj"][0])
            put("inw", _tile_stationary(W[:, :16384], KC, 128))
            dtw = np.ascontiguousarray(W[:, 16384:].reshape(KC, 128, 192).transpose(1, 0, 2))
            put("dtw", dtw.reshape(nsh(), 128, -1))
            put("outw", _tile_stationary(np.asarray(inp["hyb_out_proj"][0]), 64, KC))
            cl, sl, cdsd = _dft_tables()
            put("dftc", cl)
            put("dfts", sl)
            shared["cdsd"] = cdsd
            shared["mn0"] = _pc(inp["mix_norm"][0], KC)
            cw = np.asarray(inp["ssd_conv_w"][0], np.float32)
            shared["cw"] = np.ascontiguousarray(cw.reshape(5, 64, 128).transpose(2, 1, 0))
            shared["cb"] = _pc(inp["ssd_conv_b"][0], 64)
            shared["dtb"] = _rep(inp["ssd_dt_bias"][0])
            shared["alog"] = _rep(inp["ssd_A_log"][0])
            shared["dsk"] = _rep(inp["ssd_D"][0])
            shared["gnw"] = _pc(inp["ssd_gnorm"][0], 48)
        elif t == "mix1":
            put("qkvw", _tile_stationary(np.asarray(inp["attn_w_qkv"][0]), KC, 48))
            put("wow", _tile_stationary(np.asarray(inp["attn_w_o"][0]), KC, KC))
            shared["mn1"] = _pc(inp["mix_norm"][1], KC)
            shared["qkn"] = np.ascontiguousarray(np.stack([np.asarray(inp["attn_q_norm"][0], np.float32),
                                                           np.asarray(inp["attn_k_norm"][0], np.float32)], axis=1))
            shared["cos"], shared["sin"] = _rope_tables()
    if cfg.get("final_norm", True):
        shared["finw"] = _rep(inp["final_norm"])
    in_maps = []
    for c in range(NCORE):
        m = dict(shared)
        m.update(per_core[c])
        m["x"] = np.ascontiguousarray(x[c])
        in_maps.append(m)
    if cfg.get('return_prog'):
        return nc, in_maps
    res = run_bass_kernel_spmd(nc, in_maps, core_ids=list(range(NCORE)))
    return np.stack([np.asarray(r["out"]) for r in res.results], axis=0)
```
